# Optimizing a Trainium2 kernel written in Bass

```python
import jax, jax.numpy as jnp
from jax import lax
import numpy as np

D_MODEL = 1024
BATCH = 8
SEQ = 2048
DEPTH = 2
DEC_BATCH = 128
DEC_SEQ = 4
PAST_LEN = 16384
PAGE_SIZE = 128

N_META = 16
CHUNK = 64
N_MIXERS = 4
GROUP_WIDTH = D_MODEL // N_MIXERS
N_HEADS = 4
HEAD_DIM = GROUP_WIDTH // N_HEADS
GLA_DK = HEAD_DIM // 2
GLA_RANK = 16
GLA_GATE_NORM = 16.0
CONV_W = 4
D_FF = -((-8 * D_MODEL) // (3 * 256)) * 256
ALPHA = (2 * DEPTH) ** 0.25
BETA = (8 * DEPTH) ** -0.25
RET_THETA_BASE = 10000.0
LN_EPS = 1e-5
NORM_EPS = 1e-6
GATE_CLAMP = 1e-6
IN_SPLITS = (GROUP_WIDTH, GROUP_WIDTH, GROUP_WIDTH, N_HEADS, N_HEADS, GROUP_WIDTH,
             GROUP_WIDTH, GROUP_WIDTH, GROUP_WIDTH, GROUP_WIDTH,
             N_HEADS * GLA_DK, N_HEADS * GLA_DK, GROUP_WIDTH, GLA_RANK, GROUP_WIDTH,
             GROUP_WIDTH, GROUP_WIDTH, GROUP_WIDTH, GROUP_WIDTH)
VALUE_SPLITS = (2, 8, 12, 17)
D_IN = sum(IN_SPLITS)

kernel_name = "hymba_delta_hgrn2_gla_retnet_step"


def _layer_norm(x, g, b):
    xf = x.astype(jnp.float32)
    mu = jnp.mean(xf, -1, keepdims=True)
    var = jnp.mean(jnp.square(xf - mu), -1, keepdims=True)
    return ((xf - mu) * lax.rsqrt(var + LN_EPS) * g + b).astype(x.dtype)


def _head_rms(o, g):
    o = o * lax.rsqrt(jnp.mean(jnp.square(o), -1, keepdims=True) + NORM_EPS)
    return o.reshape(o.shape[:2] + (-1,)) * g


def _head_group_norm(o, g):
    mu = jnp.mean(o, -1, keepdims=True)
    var = jnp.mean(jnp.square(o - mu), -1, keepdims=True)
    o = (o - mu) * lax.rsqrt(var + LN_EPS)
    return o.reshape(o.shape[:2] + (-1,)) * g


def _l2norm(x):
    x = x.astype(jnp.float32)
    return x * lax.rsqrt(jnp.sum(jnp.square(x), -1, keepdims=True) + NORM_EPS)


def _masked_exp(diff, mask):
    return jnp.where(mask, jnp.exp(jnp.where(mask, diff, 0.0)), 0.0)


def _rope(x, pos):
    half = x.shape[-1] // 2
    inv = 1.0 / (RET_THETA_BASE ** jnp.linspace(0.0, 1.0, half, dtype=jnp.float32))
    ang = pos[:, None] * inv[None, :]
    cos = jnp.cos(ang)[None, :, None, :]
    sin = jnp.sin(ang)[None, :, None, :]
    x1, x2 = x[..., :half], x[..., half:]
    return jnp.concatenate([x1 * cos - x2 * sin, x1 * sin + x2 * cos], -1)


def _gla_chunk(s, inp):
    q, k, v, g = (a.astype(jnp.float32) for a in inp)
    c = q.shape[1]
    b = jnp.cumsum(g, axis=1)
    incl = jnp.tril(jnp.ones((c, c), bool))[None, :, :, None, None]
    diff = b[:, :, None] - b[:, None, :]
    dec = _masked_exp(diff, incl)
    att = jnp.einsum('bthk,bshk,btshk->bhts', q, k, dec)
    o = (jnp.einsum('bthk,bhkv->bthv', q * jnp.exp(b), s)
         + jnp.einsum('bhts,bshv->bthv', att, v))
    b_last = b[:, -1]
    s_new = (jnp.exp(b_last)[..., None] * s
             + jnp.einsum('bshk,bshv->bhkv', k * jnp.exp(b_last[:, None] - b), v))
    return s_new, o


def _delta_chunk(s, inp):
    q, k, v, beta, g = (a.astype(jnp.float32) for a in inp)
    c = q.shape[1]
    b = jnp.cumsum(g, axis=1)
    incl = jnp.tril(jnp.ones((c, c), bool))
    strict = jnp.tril(jnp.ones((c, c), bool), -1)
    diff = (b[:, :, None] - b[:, None, :]).transpose(0, 3, 1, 2)
    dec = _masked_exp(diff, incl[None, None])
    kk = jnp.einsum('bthk,bshk->bhts', k, k)
    qk = jnp.einsum('bthk,bshk->bhts', q, k)
    a_mat = jnp.where(strict, beta.transpose(0, 2, 1)[..., None] * dec * kk, 0.0)
    gam = jnp.exp(b)
    rhs = (beta[..., None] * (v - gam[..., None] * jnp.einsum('bthk,bhkv->bthv', k, s))).transpose(0, 2, 1, 3)
    u = lax.linalg.triangular_solve(a_mat + jnp.eye(c, dtype=jnp.float32), rhs,
                                    left_side=True, lower=True, unit_diagonal=True)
    o = (gam[..., None] * jnp.einsum('bthk,bhkv->bthv', q, s)
         + jnp.einsum('bhts,bhsv->bthv', dec * qk, u))
    b_last = b[:, -1]
    s_new = (jnp.exp(b_last)[..., None, None] * s
             + jnp.einsum('bshk,bhsv->bhkv', k * jnp.exp(b_last[:, None] - b)[..., None], u))
    return s_new, o


def _segmented(step, s0, xs, lead, chunk):
    s = s0.astype(jnp.float32)
    outs = []
    if lead > 0:
        s, o = step(s, tuple(a[:, :lead] for a in xs))
        outs.append(o)
        xs = tuple(a[:, lead:] for a in xs)
    bsz, t = xs[0].shape[:2]
    n = t // chunk
    blocks = tuple(jnp.moveaxis(a.reshape((bsz, n, chunk) + a.shape[2:]), 1, 0) for a in xs)
    s, o = lax.scan(step, s, blocks)
    outs.append(jnp.moveaxis(o, 0, 1).reshape((bsz, t) + o.shape[3:]))
    return s.astype(s0.dtype), jnp.concatenate(outs, axis=1)


def _mixers(h, conv_buf, s_delta, s_hgrn, s_gla, s_ret, pos, lead, chunk, lb,
            w_in, conv_w, a_log, dt_bias, delta_g, hgrn_g, gla_wg, gla_bg, gla_g, ret_g, w_out):
    bsz, t, _ = h.shape
    f32 = jnp.float32
    heads = lambda a: a.reshape(bsz, t, N_HEADS, -1)
    silu = jax.nn.silu
    proj = h @ w_in
    (a_q, a_k, a_v, a_beta, a_alpha, a_gate, b_q, b_f, b_i, b_gate,
     c_q, c_k, c_v, c_lr, c_gate, d_q, d_k, d_v, d_gate) = jnp.split(
        proj, np.cumsum(IN_SPLITS)[:-1].tolist(), axis=-1)

    conv_in = jnp.concatenate([conv_buf.astype(proj.dtype),
                               jnp.concatenate([a_q, a_k, a_v], -1)], axis=1)
    new_conv = conv_in[:, conv_in.shape[1] - (CONV_W - 1):]
    conv_out = silu(sum(conv_in[:, j:j + t] * conv_w[j] for j in range(CONV_W)))
    aq, ak, av = jnp.split(conv_out, 3, axis=-1)
    aq = _l2norm(heads(aq)) * HEAD_DIM ** -0.5
    ak = _l2norm(heads(ak))
    beta = jax.nn.sigmoid(a_beta.astype(f32))
    log_alpha = -jnp.exp(a_log.astype(f32)) * jax.nn.softplus(a_alpha.astype(f32) + dt_bias)
    s_a, o_a = _segmented(_delta_chunk, s_delta, (aq, ak, heads(av), beta, log_alpha), lead, chunk)
    o_a = _head_rms(o_a, delta_g) * silu(a_gate.astype(f32))

    z = heads(b_f).astype(f32)
    lbh = lb.reshape(N_HEADS, HEAD_DIM).astype(f32)
    b_k = (1.0 - lbh) * jax.nn.sigmoid(-z)
    log_f = jnp.log1p(-jnp.clip(b_k, 0.0, 1.0 - GATE_CLAMP))
    s_b, o_b = _segmented(_gla_chunk, s_hgrn, (silu(heads(b_q)), b_k, heads(b_i), log_f), lead, chunk)
    o_b = _head_rms(o_b, hgrn_g) * silu(b_gate.astype(f32))

    c_logg = jax.nn.log_sigmoid((c_lr @ gla_wg + gla_bg).astype(f32)) / GLA_GATE_NORM
    s_c, o_c = _segmented(_gla_chunk, s_gla,
                          (heads(c_q) * GLA_DK ** -0.5, heads(c_k), heads(c_v), heads(c_logg)),
                          lead, chunk)
    o_c = _head_rms(o_c, gla_g) * silu(c_gate.astype(f32))

    dq = _rope(heads(d_q).astype(f32), pos)
    dk = _rope(heads(d_k).astype(f32), pos) * HEAD_DIM ** -0.5
    log_gamma = jnp.log1p(-jnp.exp2(-5.0 - jnp.arange(N_HEADS, dtype=f32)))
    d_logg = jnp.broadcast_to(log_gamma[:, None], dq.shape)
    s_d, o_d = _segmented(_gla_chunk, s_ret, (dq, dk, heads(d_v), d_logg), lead, chunk)
    o_d = _head_group_norm(o_d, ret_g) * silu(d_gate.astype(f32))

    o = jnp.concatenate([o_a, o_b, o_c, o_d], -1).astype(h.dtype) @ w_out
    return o, new_conv, s_a, s_b, s_c, s_d


def _trunk(x, conv_bufs, s_delta, s_hgrn, s_gla, s_ret, pos, lead, chunk, lb,
           w_in, conv_w, delta_a_log, delta_dt_bias, delta_norm_g, hgrn_norm_g,
           gla_w_gate, gla_b_gate, gla_norm_g, ret_norm_g, w_out, ln1_g, ln1_b,
           w_ffn_gate, w_ffn_up, w_ffn_down, ln2_g, ln2_b):
    new = ([], [], [], [], [])
    for l in range(DEPTH):
        m, *st = _mixers(x, conv_bufs[l], s_delta[l], s_hgrn[l], s_gla[l], s_ret[l], pos, lead, chunk,
                         lb[l], w_in[l], conv_w[l], delta_a_log[l], delta_dt_bias[l], delta_norm_g[l],
                         hgrn_norm_g[l], gla_w_gate[l], gla_b_gate[l], gla_norm_g[l], ret_norm_g[l],
                         w_out[l])
        for lst, s in zip(new, st):
            lst.append(s)
        x = _layer_norm(ALPHA * x + m, ln1_g[l], ln1_b[l])
        f = (jax.nn.silu(x @ w_ffn_gate[l]) * (x @ w_ffn_up[l])) @ w_ffn_down[l]
        x = _layer_norm(ALPHA * x + f, ln2_g[l], ln2_b[l])
    return (x,) + tuple(jnp.stack(lst) for lst in new)


def setup_inputs(seed: int = 0) -> dict:
    key = jax.random.key(seed)
    ks = jax.random.split(key, 32)
    f32 = jnp.float32
    nrm = lambda k, shape, scale: jax.random.normal(k, shape, f32) * scale
    col_scale = jnp.concatenate([jnp.full((n,), BETA if i in VALUE_SPLITS else 1.0, f32)
                                 for i, n in enumerate(IN_SPLITS)])
    dt = jnp.exp(jax.random.uniform(ks[13], (DEPTH, N_HEADS), f32, np.log(1e-3), np.log(1e-1)))
    return {
        "x_prompt": nrm(ks[0], (BATCH, SEQ, D_MODEL), 1.0),
        "x_sample": nrm(ks[1], (DEC_BATCH, DEC_SEQ, D_MODEL), 1.0),
        "state_delta_conv": nrm(ks[2], (DEPTH, DEC_BATCH, CONV_W - 1, 3 * GROUP_WIDTH), 1.0),
        "state_delta": nrm(ks[3], (DEPTH, DEC_BATCH, N_HEADS, HEAD_DIM, HEAD_DIM), 0.1),
        "state_hgrn": nrm(ks[4], (DEPTH, DEC_BATCH, N_HEADS, HEAD_DIM, HEAD_DIM), 0.5),
        "state_gla": nrm(ks[5], (DEPTH, DEC_BATCH, N_HEADS, GLA_DK, HEAD_DIM), 0.5),
        "state_ret": nrm(ks[6], (DEPTH, DEC_BATCH, N_HEADS, HEAD_DIM, HEAD_DIM), 0.5),
        "meta_tokens": nrm(ks[7], (N_META, D_MODEL), 1.0),
        "emb_ln_g": 1.0 + nrm(ks[8], (D_MODEL,), 0.02),
        "emb_ln_b": nrm(ks[9], (D_MODEL,), 0.02),
        "w_in": nrm(ks[10], (DEPTH, D_MODEL, D_IN), D_MODEL ** -0.5) * col_scale,
        "conv_w": nrm(ks[11], (DEPTH, CONV_W, 3 * GROUP_WIDTH), CONV_W ** -0.5),
        "delta_a_log": jnp.log(jax.random.uniform(ks[12], (DEPTH, N_HEADS), f32, 1.0, 16.0)),
        "delta_dt_bias": dt + jnp.log(-jnp.expm1(-dt)),
        "delta_norm_g": 1.0 + nrm(ks[14], (DEPTH, GROUP_WIDTH), 0.02),
        "hgrn_lb_logits": nrm(ks[15], (DEPTH, GROUP_WIDTH), 1.0),
        "hgrn_norm_g": 1.0 + nrm(ks[16], (DEPTH, GROUP_WIDTH), 0.02),
        "gla_w_gate": nrm(ks[17], (DEPTH, GLA_RANK, N_HEADS * GLA_DK), GLA_RANK ** -0.5),
        "gla_b_gate": nrm(ks[18], (DEPTH, N_HEADS * GLA_DK), 0.1),
        "gla_norm_g": 1.0 + nrm(ks[19], (DEPTH, GROUP_WIDTH), 0.02),
        "ret_norm_g": 1.0 + nrm(ks[20], (DEPTH, GROUP_WIDTH), 0.02),
        "w_out": nrm(ks[21], (DEPTH, N_MIXERS * GROUP_WIDTH, D_MODEL), D_MODEL ** -0.5) * BETA,
        "ln1_g": 1.0 + nrm(ks[22], (DEPTH, D_MODEL), 0.02),
        "ln1_b": nrm(ks[23], (DEPTH, D_MODEL), 0.02),
        "w_ffn_gate": nrm(ks[24], (DEPTH, D_MODEL, D_FF), D_MODEL ** -0.5) * BETA,
        "w_ffn_up": nrm(ks[25], (DEPTH, D_MODEL, D_FF), D_MODEL ** -0.5) * BETA,
        "w_ffn_down": nrm(ks[26], (DEPTH, D_FF, D_MODEL), D_FF ** -0.5) * BETA,
        "ln2_g": 1.0 + nrm(ks[27], (DEPTH, D_MODEL), 0.02),
        "ln2_b": nrm(ks[28], (DEPTH, D_MODEL), 0.02),
    }


def reference(x_prompt, x_sample, state_delta_conv, state_delta, state_hgrn, state_gla, state_ret,
              meta_tokens, emb_ln_g, emb_ln_b, w_in, conv_w, delta_a_log, delta_dt_bias,
              delta_norm_g, hgrn_lb_logits, hgrn_norm_g, gla_w_gate, gla_b_gate, gla_norm_g,
              ret_norm_g, w_out, ln1_g, ln1_b, w_ffn_gate, w_ffn_up, w_ffn_down, ln2_g, ln2_b):
    f32 = jnp.float32
    p = jax.nn.softmax(hgrn_lb_logits.astype(f32), axis=0)
    lb = jnp.cumsum(p, axis=0) - p[0]
    weights = (lb, w_in, conv_w, delta_a_log, delta_dt_bias, delta_norm_g, hgrn_norm_g,
               gla_w_gate, gla_b_gate, gla_norm_g, ret_norm_g, w_out, ln1_g, ln1_b,
               w_ffn_gate, w_ffn_up, w_ffn_down, ln2_g, ln2_b)

    bsz = x_prompt.shape[0]
    meta = jnp.broadcast_to(meta_tokens.astype(x_prompt.dtype)[None], (bsz, N_META, D_MODEL))
    xp = _layer_norm(jnp.concatenate([meta, x_prompt], axis=1), emb_ln_g, emb_ln_b)
    pos_p = jnp.arange(N_META + x_prompt.shape[1], dtype=f32)
    conv0 = jnp.zeros((DEPTH, bsz, CONV_W - 1, 3 * GROUP_WIDTH), x_prompt.dtype)
    sq0 = jnp.zeros((DEPTH, bsz, N_HEADS, HEAD_DIM, HEAD_DIM), f32)
    sg0 = jnp.zeros((DEPTH, bsz, N_HEADS, GLA_DK, HEAD_DIM), f32)
    yp, conv_p, delta_p, hgrn_p, gla_p, ret_p = _trunk(
        xp, conv0, sq0, sq0, sg0, sq0, pos_p, N_META, CHUNK, *weights)
    y_prompt = yp[:, N_META:]

    xs = _layer_norm(x_sample, emb_ln_g, emb_ln_b)
    pos_s = PAST_LEN + jnp.arange(x_sample.shape[1], dtype=f32)
    y_sample, conv_s, delta_s, hgrn_s, gla_s, ret_s = _trunk(
        xs, state_delta_conv, state_delta, state_hgrn, state_gla, state_ret,
        pos_s, 0, x_sample.shape[1], *weights)

    return (y_prompt, y_sample, conv_p, conv_s, delta_p, delta_s, hgrn_p, hgrn_s,
            gla_p, gla_s, ret_p, ret_s)
```

```python
import numpy as np
from contextlib import ExitStack
import concourse.bass as bass
import concourse.mybir as mybir
from concourse.bass_utils import run_bass_kernel_spmd

F32 = mybir.dt.float32
BF16 = mybir.dt.bfloat16
AF = mybir.ActivationFunctionType
ALU = mybir.AluOpType
AX = mybir.AxisListType

D = 1024
DIN = 3864
DFF = 2816
NL = 2
SEQ = 2048
NPAD = 48
NMETA = 16
NPROMPT = NPAD + NMETA + SEQ
NSAMP = 64
NTOK = NPROMPT + NSAMP
NT = 256
NTILES = 9
ALPHA = (2 * NL) ** 0.25
LN_EPS = 1e-5
NORM_EPS = 1e-6
PAST_LEN = 16384
N_DMA_SEMS = 16


class Tracker:
    def __init__(self, nc, stack):
        self.nc = nc
        self.eng = {"pe": nc.tensor, "act": nc.scalar, "dve": nc.vector,
                    "pool": nc.gpsimd, "sp": nc.sync}
        self.sem = {k: stack.enter_context(nc.semaphore("s_" + k)) for k in self.eng}
        self.cnt = {k: 0 for k in self.eng}
        self.dsem = {q: [stack.enter_context(nc.semaphore(f"d_{q}{i}")) for i in range(N_DMA_SEMS)]
                     for q in ("sp", "pool")}
        self.dcnt = {q: [0] * N_DMA_SEMS for q in self.dsem}
        self.dnext = {q: 0 for q in self.dsem}
        self.seen = {}
        self.lastw = {}
        self.readers = {}
        self.n_wait = 0
        self.n_ins = 0

    @staticmethod
    def _key(x):
        if isinstance(x, tuple):
            return (x[0].name, x[1])
        return (x.name, None)

    def _wait(self, e, tok):
        sem, val, src = tok
        k = (e, sem.name)
        if self.seen.get(k, 0) >= val:
            return
        self.seen[k] = val
        self.eng[e].wait_ge(sem, val)
        self.n_wait += 1

    def _deps(self, e, rk, wk, prk=()):
        for k in rk:
            t = self.lastw.get(k)
            if t is not None:
                self._wait(e, t)
        for k in prk:
            t = self.lastw.get(k)
            if t is not None:
                self._wait(e, t)
            for r in self.readers.get(k, ()):
                if r[2] != e:
                    self._wait(e, r)
        for k in wk:
            t = self.lastw.get(k)
            if t is not None and not (t[2] == e and e == "pe"):
                self._wait(e, t)
            for r in self.readers.get(k, ()):
                self._wait(e, r)

    def _commit(self, tok, rk, wk):
        for k in wk:
            self.lastw[k] = tok
            self.readers[k] = []
        for k in rk:
            if k not in wk:
                self.readers.setdefault(k, []).append(tok)

    def op(self, e, fn, reads, writes):
        rk, prk = [], []
        wk = [self._key(x) for x in writes]
        for x in reads:
            k = self._key(x)
            if k in wk:
                continue
            (prk if k[0].startswith("ps") else rk).append(k)
        self._deps(e, rk, wk, prk)
        ins = fn(self.eng[e])
        self.cnt[e] += 1
        ins.then_inc(self.sem[e], 1)
        self._commit((self.sem[e], self.cnt[e], e), rk + prk, wk)
        self.n_ins += 1
        return ins

    def dma(self, out, in_, q="sp", **kw):
        ok = self._key(out)
        ik = self._key(in_)
        oap = out[0] if isinstance(out, tuple) else out
        iap = in_[0] if isinstance(in_, tuple) else in_
        i = self.dnext[q]
        self.dnext[q] = (i + 1) % N_DMA_SEMS
        sem = self.dsem[q][i]
        if self.dcnt[q][i] > 0:
            self._wait(q, (sem, self.dcnt[q][i], None))
        self._deps(q, [ik], [ok])
        ins = self.eng[q].dma_start(out=oap, in_=iap, **kw)
        self.dcnt[q][i] += 16
        ins.then_inc(sem, 16)
        self._commit((sem, self.dcnt[q][i], None), [ik], [ok])
        self.n_ins += 1
        return ins

    def finish(self):
        for q in self.dsem:
            for i, sem in enumerate(self.dsem[q]):
                if self.dcnt[q][i] > 0:
                    self._wait("sp", (sem, self.dcnt[q][i], None))
        for e in self.eng:
            if e != "sp" and self.cnt[e] > 0:
                self._wait("sp", (self.sem[e], self.cnt[e], e))

    def matmul(self, out, lhsT, rhs, start=True, stop=True):
        return self.op("pe", lambda e: e.matmul(out, lhsT, rhs, start=start, stop=stop),
                       [lhsT, rhs], [out])

    def transpose(self, out, in_, ident):
        return self.op("pe", lambda e: e.transpose(out, in_, ident), [in_, ident], [out])

    def act(self, out, in_, func, bias=None, scale=None):
        kw = {}
        rd = [in_]
        if bias is not None:
            kw["bias"] = bias
            if not isinstance(bias, (int, float)):
                rd.append(bias)
        if scale is not None:
            kw["scale"] = scale
            if not isinstance(scale, (int, float)):
                rd.append(scale)
        return self.op("act", lambda e: e.activation(out, in_, func, **kw), rd, [out])

    def tt(self, out, in0, in1, op, eng="dve"):
        return self.op(eng, lambda e: e.tensor_tensor(out, in0, in1, op), [in0, in1], [out])

    def ts(self, out, in0, s1, op0, s2=None, op1=None, eng="dve"):
        rd = [in0]
        if not isinstance(s1, (int, float)):
            rd.append(s1)
        if s2 is not None and not isinstance(s2, (int, float)):
            rd.append(s2)
        if op1 is None:
            return self.op(eng, lambda e: e.tensor_scalar(out, in0, s1, None, op0), rd, [out])
        return self.op(eng, lambda e: e.tensor_scalar(out, in0, s1, s2, op0, op1), rd, [out])

    def stt(self, out, in0, scalar, in1, op0, op1, eng="dve"):
        rd = [in0, in1]
        if not isinstance(scalar, (int, float)):
            rd.append(scalar)
        return self.op("dve", lambda e: e.scalar_tensor_tensor(out, in0, scalar, in1, op0, op1), rd, [out])

    def copy(self, out, in_, eng="dve"):
        if eng == "act":
            return self.op(eng, lambda e: e.copy(out, in_), [in_], [out])
        return self.op(eng, lambda e: e.tensor_copy(out, in_), [in_], [out])

    def memset(self, out, val, eng="dve"):
        return self.op(eng, lambda e: e.memset(out, val), [], [out])

    def scan(self, out, d0, eng="dve"):
        return self.op(eng, lambda e: e.tensor_tensor_scan(out, d0, d0, 0.0, ALU.add, ALU.bypass),
                       [d0], [out])

    def recip(self, out, in_):
        return self.op("dve", lambda e: e.reciprocal(out, in_), [in_], [out])


C128_LAYOUT = [("ident", 128), ("I2", 64), ("onesM", 128), ("bones", 128), ("rot", 128),
               ("mTi", 128), ("mTs", 128), ("smTi", 128), ("smTs", 128),
               ("rm64", 2), ("rm32", 4), ("rm16", 16), ("lg", 2), ("one", 1)]


def _c128_offsets():
    off = {}
    o = 0
    for n, w in C128_LAYOUT:
        off[n] = (o, w)
        o += w
    return off, o


def make_consts():
    off, tot = _c128_offsets()
    c = np.zeros((128, tot), np.float32)

    def put(name, arr):
        o, w = off[name]
        c[:, o:o + w] = arr
    p = np.arange(128)
    put("ident", np.eye(128, dtype=np.float32))
    put("I2", (p[:, None] % 64 == np.arange(64)[None, :]).astype(np.float32))
    put("onesM", np.full((128, 128), 1.0 / D, np.float32))
    put("bones", (p[:, None] // 64 == p[None, :] // 64).astype(np.float32))
    rot = np.zeros((128, 128), np.float32)
    for i in range(128):
        d = i % 64
        j = i + 32 if d < 32 else i - 32
        rot[i, j] = 1.0
    put("rot", rot)
    hl = p // 64
    s = p % 64
    same = hl[:, None] == hl[None, :]
    put("mTi", (same & (s[:, None] <= s[None, :])).astype(np.float32))
    put("mTs", (same & (s[:, None] < s[None, :])).astype(np.float32))
    sq = s // 4
    tau = s % 4
    sames = same & (sq[:, None] == sq[None, :])
    put("smTi", (sames & (tau[:, None] <= tau[None, :])).astype(np.float32))
    put("smTs", (sames & (tau[:, None] < tau[None, :])).astype(np.float32))
    put("rm64", (p[:, None] // 64 == np.arange(2)[None, :]).astype(np.float32))
    put("rm32", (p[:, None] // 32 == np.arange(4)[None, :]).astype(np.float32))
    put("rm16", (sq[:, None] == np.arange(16)[None, :]).astype(np.float32))
    lgam = np.log1p(-np.exp2(-5.0 - np.arange(4, dtype=np.float32))).astype(np.float32)
    lg = np.zeros((128, 2), np.float32)
    for pr in range(2):
        lg[:, pr] = lgam[2 * pr + hl]
    put("lg", lg)
    put("one", np.ones((128, 1), np.float32))
    c4 = np.zeros((4, 4 + 256 + 128 + 1), np.float32)
    c4[:, 0:4] = np.eye(4)
    for h in range(4):
        pr, hl_ = h // 2, h % 2
        c4[h, 4 + pr * 128 + hl_ * 64: 4 + pr * 128 + hl_ * 64 + 64] = 1.0
    c4[:, 260:388] = 1.0
    c4[:, 388] = 1.0
    c8 = np.zeros((8, 4), np.float32)
    for j in range(4):
        c8[4 + j, j] = 1.0
    pos = np.zeros(NTOK, np.float32)
    pos[:NPROMPT] = np.maximum(np.arange(NPROMPT) - NPAD, 0).astype(np.float32)
    pos[NPROMPT:] = (PAST_LEN + (np.arange(NSAMP) % 4)).astype(np.float32)
    inv = (1.0 / (np.float32(10000.0) ** np.linspace(0.0, 1.0, 32, dtype=np.float32))).astype(np.float32)
    ang = (pos[:, None] * inv[None, :]).astype(np.float32)
    cos = np.cos(ang).astype(np.float32)
    sin = np.sin(ang).astype(np.float32)
    d = p % 64
    cosT = cos[:, d % 32].T.copy()
    sinT = (sin[:, d % 32].T * np.where(d < 32, -1.0, 1.0)[:, None]).astype(np.float32)
    return dict(c128=c, c4=c4, c8=c8, cosT=np.ascontiguousarray(cosT),
                sinT=np.ascontiguousarray(sinT))


SLABS = [
    (0, 256, [("aq0", 0, 128), ("aq1", 128, 128)]),
    (256, 512, [("ak0", 256, 128), ("ak1", 384, 128)]),
    (512, 768, [("av0", 512, 128), ("av1", 640, 128)]),
    (768, 904, [("aba", 768, 8), ("ag0", 776, 128)]),
    (904, 1160, [("ag1", 904, 128), ("bq0", 1032, 128)]),
    (1160, 1416, [("bq1", 1160, 128), ("bf0", 1288, 128)]),
    (1416, 1672, [("bf1", 1416, 128), ("bi0", 1544, 128)]),
    (1672, 1928, [("bi1", 1672, 128), ("bg0", 1800, 128)]),
    (1928, 2184, [("bg1", 1928, 128), ("cq", 2056, 128)]),
    (2184, 2440, [("ck", 2184, 128), ("cv0", 2312, 128)]),
    (2440, 2584, [("cv1", 2440, 128), ("clr", 2568, 16)]),
    (2584, 2840, [("cg0", 2584, 128), ("cg1", 2712, 128)]),
    (2840, 3096, [("dq0", 2840, 128), ("dq1", 2968, 128)]),
    (3096, 3352, [("dk0", 3096, 128), ("dk1", 3224, 128)]),
    (3352, 3608, [("dv0", 3352, 128), ("dv1", 3480, 128)]),
    (3608, 3864, [("dg0", 3608, 128), ("dg1", 3736, 128)]),
]

PP128 = [("embg", 8), ("embb", 8), ("ln1g", 16), ("ln1b", 16), ("ln2g", 16), ("ln2b", 16),
         ("convw", 48), ("normg", 16), ("lbl", 4), ("bgc", 2)]


def _pp_offsets():
    off = {}
    o = 0
    for n, w in PP128:
        off[n] = (o, w)
        o += w
    return off, o


def pack_params(inp):
    off, tot = _pp_offsets()
    a = np.zeros((128, tot), np.float32)

    def put(name, arr):
        o, w = off[name]
        a[:, o:o + w] = np.asarray(arr, np.float32).reshape(128, w)
    fm = lambda v: np.asarray(v, np.float32).reshape(8, 128).T
    put("embg", fm(inp["emb_ln_g"]))
    put("embb", fm(inp["emb_ln_b"]))
    for nm, key in (("ln1g", "ln1_g"), ("ln1b", "ln1_b"), ("ln2g", "ln2_g"), ("ln2b", "ln2_b")):
        put(nm, np.stack([fm(inp[key][l]) for l in range(NL)], axis=1))
    cw = np.asarray(inp["conv_w"], np.float32).reshape(NL, 4, 6, 128)
    put("convw", cw.transpose(3, 0, 2, 1))
    ng = np.stack([np.asarray(inp[k], np.float32).reshape(NL, 2, 128)
                   for k in ("delta_norm_g", "hgrn_norm_g", "gla_norm_g", "ret_norm_g")], axis=1)
    put("normg", ng.transpose(3, 0, 1, 2))
    lb = np.asarray(inp["hgrn_lb_logits"], np.float32).reshape(NL, 2, 128)
    put("lbl", lb.transpose(2, 0, 1))
    put("bgc", np.asarray(inp["gla_b_gate"], np.float32).T)
    pp4 = np.concatenate([np.asarray(inp["delta_a_log"], np.float32).T,
                          np.asarray(inp["delta_dt_bias"], np.float32).T], axis=1)
    pp16 = np.asarray(inp["gla_w_gate"], np.float32).transpose(1, 0, 2).reshape(16, NL * 128)
    return a, np.ascontiguousarray(pp4), np.ascontiguousarray(pp16)


def build_program(nlayers=NL, dbg=None):
    nc = bass.Bass("TRN2", target_bir_lowering=False)
    c_off, c_tot = _c128_offsets()
    p_off, p_tot = _pp_offsets()

    def din(name, shape):
        return nc.dram_tensor(name, list(shape), F32, kind="ExternalInput").ap()

    def dout(name, shape):
        return nc.dram_tensor(name, list(shape), F32, kind="ExternalOutput").ap()

    xp_d = din("xp", [SEQ, D])
    xs_d = din("xs", [NSAMP, D])
    meta_d = din("meta", [NMETA, D])
    sconv_d = din("sconv", [NL, 48, 768])
    sdelta_d = din("sdelta", [NL, 16, 4, 64, 64])
    shgrn_d = din("shgrn", [NL, 16, 4, 64, 64])
    sgla_d = din("sgla", [NL, 16, 4, 32, 64])
    sret_d = din("sret", [NL, 16, 4, 64, 64])
    win_d = din("w_in", [NL, D, DIN])
    wout_d = din("w_out", [NL, D, D])
    wg_d = din("w_g", [NL, D, DFF])
    wu_d = din("w_u", [NL, D, DFF])
    wd_d = din("w_d", [NL, DFF, D])
    pp128_d = din("pp128", [128, p_tot])
    pp4_d = din("pp4", [4, 4])
    pp16_d = din("pp16", [16, NL * 128])
    c128_d = din("c128", [128, c_tot])
    c4_d = din("c4", [4, 389])
    c8_d = din("c8", [8, 4])
    cos_d = din("cosT", [128, NTOK])
    sin_d = din("sinT", [128, NTOK])

    yp_d = dout("y_p", [SEQ, D])
    ys_d = dout("y_s", [NSAMP, D])
    convp_d = dout("conv_p", [NL, 3, 768])
    convs_d = dout("conv_s", [NL, 16, 3, 768])
    sp_out = {"A": dout("delta_p", [NL, 4, 64, 64]), "B": dout("hgrn_p", [NL, 4, 64, 64]),
              "C": dout("gla_p", [NL, 4, 32, 64]), "D": dout("ret_p", [NL, 4, 64, 64])}
    ss_out = {"A": dout("delta_s", [NL, 16, 4, 64, 64]), "B": dout("hgrn_s", [NL, 16, 4, 64, 64]),
              "C": dout("gla_s", [NL, 16, 4, 32, 64]), "D": dout("ret_s", [NL, 16, 4, 64, 64])}
    ss_in = {"A": sdelta_d, "B": shgrn_d, "C": sgla_d, "D": sret_d}
    dbg_d = dout("dbg", [128, 4096]) if dbg else None

    SLAB_LIST = []
    for (c0_, c1_, _chs) in SLABS:
        SLAB_LIST.append(("in", c0_, c1_, 0, 8))
    for h_ in range(4):
        SLAB_LIST.append(("out", h_ * 256, (h_ + 1) * 256, 0, 8))
    for s_ in range(11):
        SLAB_LIST.append(("g", s_ * 256, (s_ + 1) * 256, 0, 8))
        SLAB_LIST.append(("u", s_ * 256, (s_ + 1) * 256, 0, 8))
    for m_ in range(8):
        SLAB_LIST.append(("d", m_ * 128, (m_ + 1) * 128, 0, 11))
        SLAB_LIST.append(("d", m_ * 128, (m_ + 1) * 128, 11, 11))
    SLAB_ID = {(n_, a_, k0_): i_ for i_, (n_, a_, b_, k0_, k_) in enumerate(SLAB_LIST)}
    wslab = nc.dram_tensor("wslab", [NL, len(SLAB_LIST), 128, 2048], BF16).ap()
    wsrc = {"in": win_d, "out": wout_d, "g": wg_d, "u": wu_d, "d": wd_d}

    st = ExitStack()
    with st:
        T = Tracker(nc, st)
        for l_ in range(nlayers):
            for i_, (n_, a_, b_, k0_, k_) in enumerate(SLAB_LIST):
                w_ = b_ - a_
                T.dma((wslab[l_, i_, :, 0:k_ * w_].rearrange("p (k c) -> p k c", k=k_), (l_, i_)),
                      wsrc[n_][l_].rearrange("(kc p) c -> p kc c", p=128)[:, k0_:k0_ + k_, a_:b_], q="pool")
        _n = [0]

        def sb(shape, name=None, dt=F32):
            _n[0] += 1
            return st.enter_context(nc.sbuf_tensor("sb_" + (name or f"t{_n[0]}"), list(shape), dt))

        banks = [st.enter_context(nc.psum_tensor(f"ps{i}", [128, 512], F32)) for i in range(8)]
        rr = {"big": [0, (0, 1, 2)], "prep": [0, (3, 4)], "G": [0, (5, 6)], "st6": [0, (0, 1, 2, 5, 6, 7)]}

        def bank(kind):
            i, ids = rr[kind]
            rr[kind][0] = (i + 1) % len(ids)
            return banks[ids[i]]
        B_O, B_S, B_D = banks[5], banks[6], banks[7]

        c128 = sb([128, c_tot], "c128")
        pp = sb([128, p_tot], "pp128")
        c4 = sb([4, 389], "c4")
        c8 = sb([8, 4], "c8")
        pp4 = sb([4, 4], "pp4")
        wgc = sb([16, NL * 128], "wgc")
        T.dma(c128[:], c128_d)
        T.dma(pp[:], pp128_d)
        T.dma(c4[:], c4_d)
        T.dma(c8[:], c8_d)
        T.dma(pp4[:], pp4_d)
        T.dma(wgc[:], pp16_d)

        def C(name, lo=0, hi=None, rows=slice(0, 128)):
            o, w = c_off[name]
            hi = w if hi is None else hi
            return c128[rows, o + lo:o + hi]

        def P(name, lo=0, hi=None):
            o, w = p_off[name]
            hi = w if hi is None else hi
            return pp[:, o + lo:o + hi]
        ident = C("ident")
        I2 = C("I2")
        hm = c4[:, 0:4]
        onesrow = c4[:, 260:388]
        ones4 = c4[:, 388:389]

        def hsel(pr):
            return c4[:, 4 + pr * 128: 4 + (pr + 1) * 128]

        nA = sb([4, 2], "nA")
        T.act(nA[:], pp4[:, 0:2], AF.Exp)
        T.ts(nA[:], nA[:], -1.0, ALU.mult)
        rlb = sb([128, 4], "rlb")
        T.memset(rlb[:], 1.0)
        dl = sb([128, 2], "dl")
        T.tt(dl[:], P("lbl", 2, 4), P("lbl", 0, 2), ALU.subtract)
        T.act(dl[:], dl[:], AF.Exp)
        T.ts(rlb[:, 2:4], dl[:], 1.0, ALU.add)
        negbg = sb([128, 2], "negbg")
        T.ts(negbg[:], P("bgc"), -1.0, ALU.mult)

        FW = NT + 8
        xT = sb([128, 8, NT], "xT")
        x1T = sb([128, 8, NT], "x1T")
        oT = sb([128, 8, NT], "oT", BF16)
        xTb = sb([128, 8, NT], "xTb", BF16)
        x1Tb = sb([128, 8, NT], "x1Tb", BF16)
        slabs = [sb([128, 2048], f"slab{i}", BF16) for i in range(4)]
        slab_i = [0]

        def next_slab():
            s = slabs[slab_i[0]]
            slab_i[0] = (slab_i[0] + 1) % len(slabs)
            return s
        NFM = 41
        fmt = [sb([128, FW], f"fm{i}") for i in range(NFM)]
        cb = fmt[0:6]
        cs = fmt[6:12]
        sg = {"A": fmt[12:14], "B": fmt[14:16], "C": fmt[16:18], "D": fmt[18:20]}
        qk = {("B", "q"): fmt[20:22], ("B", "k"): fmt[22:24], ("B", "g"): fmt[24:26],
              ("C", "q"): fmt[26:27], ("C", "k"): fmt[27:28], ("C", "g"): fmt[28:29],
              ("D", "q"): fmt[29:31], ("D", "k"): fmt[31:33]}
        tmp_pool = fmt[33:NFM]
        tmp_i = [0]

        def tmp():
            t_ = tmp_pool[tmp_i[0]]
            tmp_i[0] = (tmp_i[0] + 1) % len(tmp_pool)
            return t_
        tmpg_i, tmpd_i = [0], [0]

        def tmpG():
            t_ = tmp_pool[tmpg_i[0]]
            tmpg_i[0] = (tmpg_i[0] + 1) % 6
            return t_

        def tmpD():
            t_ = tmp_pool[6 + tmpd_i[0]]
            tmpd_i[0] = (tmpd_i[0] + 1) % 2
            return t_
        hT = [t_[:].bitcast(BF16) for t_ in fmt[6:28]]
        cbs = [sb([128, 16, 7], f"cbs{j}") for j in range(6)]
        hist = [sb([128, 6, 3], f"hist{l}") for l in range(nlayers)]
        ba8 = sb([8, FW], "ba8")
        lrT = sb([16, FW], "lrT")
        s4g = sb([4, FW], "s4g")
        s4cs = sb([4, FW], "s4cs")
        s4q = sb([4, 5, FW], "s4q")
        ebl4 = sb([4, 24], "ebl4")
        rowx = [sb([4, 5, 128], f"rowx{i}") for i in range(2)]
        rowx_i = [0]

        def next_rowx():
            t_ = rowx[rowx_i[0]]
            rowx_i[0] = (rowx_i[0] + 1) % 2
            return t_
        MIX = ("A", "B", "C", "D")
        Sp = {(l, m, pr): sb([128, 64], f"Sp{l}{m}{pr}") for l in range(nlayers) for m in MIX for pr in range(2)}
        for k_, v_ in Sp.items():
            T.memset(v_[:], 0.0)
        Sspool = [sb([128, 16, 64], f"Ss{i}") for i in range(2)]
        ss_i = [0]
        cstage = Sspool[0][:].rearrange("p a b -> p (a b)")

        def next_ss():
            t_ = Sspool[ss_i[0]]
            ss_i[0] = (ss_i[0] + 1) % len(Sspool)
            return t_
        vpair = {(m, pr): sb([128, 4, 64], f"vp{m}{pr}") for m in MIX for pr in range(2)}
        bdq = {pr: sb([128, 4, 128], f"bdq{pr}") for pr in range(2)}
        attT = {pr: sb([128, 4, 128], f"att{pr}") for pr in range(2)}
        kpair = {pr: sb([128, 4, 128], f"kp{pr}") for pr in range(2)}
        bdqG = {pr: sb([128, 4, 128], f"bdqG{pr}") for pr in range(2)}
        attTG = {pr: sb([128, 4, 128], f"attG{pr}") for pr in range(2)}
        kpairG = {pr: sb([128, 4, 128], f"kpG{pr}") for pr in range(2)}
        redG = sb([128, 128], "redG")
        bdtmp = [sb([128, 4, 128], f"bdt{i}") for i in range(11)]
        bdt_i = [0]

        def next_bdt():
            t_ = bdtmp[bdt_i[0]]
            bdt_i[0] = (bdt_i[0] + 1) % 9
            return t_
        bdtg_i = [0]

        def next_bdtG():
            t_ = bdtmp[9 + bdtg_i[0]]
            bdtg_i[0] = (bdtg_i[0] + 1) % 2
            return t_
        eblc = {pr: sb([128, 24], f"eblc{pr}") for pr in range(2)}
        o_allA = sb([128, 4, 2, 64], "o_allA")
        o_allG = sb([128, 4, 6, 64], "o_allG")
        xin = [o_allG[:].rearrange("p a b c -> p (a b c)")[:, 0:D]]
        vexp = [sb([128, 16, 64], f"vexp{i}") for i in range(1)]
        in_stage = [Sspool[1][:].rearrange("p a b -> p (a b)"), vexp[0][:].rearrange("p a b -> p (a b)")]
        out_stage = [Sspool[0][:].rearrange("p a b -> p (a b)"), xin[0]]
        prefetched = {}
        kTbd = {pr: sb([128, 4, 128], f"kTbd{pr}") for pr in range(2)}
        TT = {pr: sb([128, 4, 128], f"TT{pr}") for pr in range(2)}
        colsb = sb([128, 4, 2, 8], "colsb")
        eblS = {pr: sb([128, 16], f"eblS{pr}") for pr in range(2)}
        Sch = {pr: sb([128, 4, 64], f"Sch{pr}") for pr in range(2)}
        small = [sb([128, 128], f"sm{i}") for i in range(6)]
        sm_i = [0]

        def next_sm():
            t_ = small[sm_i[0]]
            sm_i[0] = (sm_i[0] + 1) % len(small)
            return t_
        cosb = sb([128, NT], "cosb")
        sinb = sb([128, NT], "sinb")
        nstatA = sb([128, 4, 2, 2], "nstatA")
        nstatG = sb([128, 4, 6, 2], "nstatG")

        def layer_norm_fm(src, dst, dstb, gname, bname, goff, nt):
            b1 = bank("big")
            b2 = bank("big")
            for kc in range(8):
                T.matmul(b1[:, 0:nt], C("onesM"), src[:, kc, 0:nt], start=(kc == 0), stop=(kc == 7))
            s2 = bdtmp[4][:].rearrange("p a b -> p (a b)")
            for kc in range(4):
                sq = tmp()
                T.tt(sq[:, 0:nt], src[:, kc, 0:nt], src[:, kc, 0:nt], ALU.mult, eng="dve")
                T.matmul(b2[:, 0:nt], C("onesM"), sq[:, 0:nt], start=(kc == 0), stop=False)
            T.tt(s2[:, 0:nt], src[:, 4, 0:nt], src[:, 4, 0:nt], ALU.mult, eng="dve")
            for kc in range(5, 8):
                sq = tmp()
                T.tt(sq[:, 0:nt], src[:, kc, 0:nt], src[:, kc, 0:nt], ALU.mult, eng="dve")
                T.tt(s2[:, 0:nt], s2[:, 0:nt], sq[:, 0:nt], ALU.add, eng="dve")
            T.matmul(b2[:, 0:nt], C("onesM"), s2[:, 0:nt], start=False, stop=True)
            mean, m2, var, rstd = (bdtmp[i_][:].rearrange("p a b -> p (a b)") for i_ in range(4))
            T.copy(mean[:, 0:nt], b1[:, 0:nt], eng="act")
            T.tt(m2[:, 0:nt], mean[:, 0:nt], mean[:, 0:nt], ALU.mult, eng="dve")
            T.tt(var[:, 0:nt], b2[:, 0:nt], m2[:, 0:nt], ALU.subtract)
            T.ts(var[:, 0:nt], var[:, 0:nt], LN_EPS, ALU.add)
            T.act(var[:, 0:nt], var[:, 0:nt], AF.Ln)
            T.act(rstd[:, 0:nt], var[:, 0:nt], AF.Exp, scale=-0.5)
            for kc in range(8):
                t1 = tmp()
                T.tt(t1[:, 0:nt], src[:, kc, 0:nt], mean[:, 0:nt], ALU.subtract)
                T.tt(t1[:, 0:nt], t1[:, 0:nt], rstd[:, 0:nt], ALU.mult, eng="dve")
                T.act(dst[:, kc, 0:nt], t1[:, 0:nt], AF.Identity, bias=P(bname, goff + kc, goff + kc + 1),
                      scale=P(gname, goff + kc, goff + kc + 1))
                T.act(dstb[:, kc, 0:nt], t1[:, 0:nt], AF.Identity, bias=P(bname, goff + kc, goff + kc + 1),
                      scale=P(gname, goff + kc, goff + kc + 1))

        def load_slab(name, l_, c0_, kc, w, k0=0):
            s_ = next_slab()
            i_ = SLAB_ID[(name, c0_, k0)]
            T.dma(s_[:, 0:kc * w], (wslab[l_, i_, :, 0:kc * w], (l_, i_)), q="sp")
            return s_[:, 0:kc * w].rearrange("p (k c) -> p k c", k=kc)

        for ti in range(NTILES):
            col0 = ti * NT
            nt = NT if ti < 8 else 128
            nreg = nt if ti < 8 else 64
            has_s = ti == 8
            nchr = nreg // 64
            segs = [(0, nchr, 64)] + ([(64, 16, 4)] if has_s else [])
            chunks = [("r", c) for c in range(nchr)] + ([("s", nchr)] if has_s else [])

            def load_block(ti_, blk_):
                xi_ = in_stage[blk_]
                c_lo_ = ti_ * NT + blk_ * 128
                if c_lo_ >= NPROMPT:
                    T.dma(xi_[0:64, :], xs_d)
                    return xi_, 64
                if ti_ == 0 and blk_ == 0:
                    T.memset(xi_[0:NPAD, :], 0.0)
                    T.dma(xi_[NPAD:NPAD + NMETA, :], meta_d)
                    T.dma(xi_[64:128, :], xp_d[0:64, :])
                elif ti_ == 8:
                    r0_ = c_lo_ - 64
                    T.dma(xi_[0:64, :], xp_d[r0_:r0_ + 64, :])
                    T.dma(xi_[64:128, :], xs_d)
                else:
                    r0_ = c_lo_ - 64
                    T.dma(xi_[:, :], xp_d[r0_:r0_ + 128, :])
                return xi_, 128

            for blk in range(nt // 128):
                if (ti, blk) in prefetched:
                    xi, nrow = prefetched.pop((ti, blk))
                else:
                    xi, nrow = load_block(ti, blk)
                for kc in range(8):
                    b = bank("big")
                    T.transpose(b[:, 0:nrow], xi[0:nrow, kc * 128:(kc + 1) * 128], ident[0:nrow, 0:nrow])
                    T.copy(x1T[:, kc, blk * 128: blk * 128 + nrow], b[:, 0:nrow], eng=("act" if kc % 2 else "dve"))
            layer_norm_fm(x1T, xT, xTb, "embg", "embb", 0, nt)
            T.dma(cosb[:, 0:nt], cos_d[:, col0:col0 + nt])
            T.dma(sinb[:, 0:nt], sin_d[:, col0:col0 + nt])

            for l in range(nlayers):
                if ti == 0:
                    T.memset(xT[:, :, 0:NPAD], 0.0)
                    T.memset(xTb[:, :, 0:NPAD], 0.0)
                    for j in range(6):
                        T.memset(cb[j][:, 0:3], 0.0)
                else:
                    for j in range(6):
                        T.copy(cb[j][:, 0:3], hist[l][:, j, :], eng="dve")
                if has_s:
                    xi = xin[0]
                    T.dma(xi[0:48, 0:768], sconv_d[l])
                    for j in range(6):
                        b = bank("big")
                        T.transpose(b[:, 0:48], xi[0:48, j * 128:(j + 1) * 128], ident[0:48, 0:48])
                        T.copy(cbs[j][:, :, 0:3], b[:, 0:48].rearrange("p (s j) -> p s j", j=3), eng="dve")

                nch = nt // 64

                def segv(t_, off, s0, ng, gl, rows=slice(0, 128)):
                    return t_[rows, off + s0: off + s0 + ng * gl].rearrange("p (g t) -> p g t", t=gl)

                def local_cumsum(gF, csb, bT, rows=slice(0, 128), np_=128):
                    T.memset(csb[rows, 0:1], 0.0)
                    T.scan(csb[rows, 1:1 + nt], gF[rows, 0:nt])
                    for (s0, ng, gl) in segs:
                        T.tt(segv(bT, 0, s0, ng, gl, rows), segv(csb, 1, s0, ng, gl, rows),
                             segv(csb, 0, s0, ng, gl, rows)[:, :, 0:1].to_broadcast([np_, ng, gl]),
                             ALU.subtract, eng="dve")

                def bl_minus_b(d3, bT, rows=slice(0, 128), np_=128):
                    for (s0, ng, gl) in segs:
                        T.tt(segv(d3, 0, s0, ng, gl, rows),
                             segv(bT, 0, s0, ng, gl, rows)[:, :, gl - 1:gl].to_broadcast([np_, ng, gl]),
                             segv(bT, 0, s0, ng, gl, rows), ALU.subtract, eng="dve")

                def exp_bl(dst, bT, rows=slice(0, 128)):
                    for (s0, ng, gl) in segs:
                        off = 0 if gl == 64 else 4
                        T.act(dst[rows, off:off + ng], segv(bT, 0, s0, ng, gl, rows)[:, :, gl - 1], AF.Exp)

                def sample_inter(lhsT_bd, Sb, red, pool="big"):
                    vx = vexp[0]
                    for h_ in range(2):
                        b = bank(pool)
                        T.matmul(b[:, 0:512], lhsT_bd, Sb[:, 8 * h_:8 * h_ + 8, :].rearrange("p s v -> p (s v)"))
                        T.tt(vx[:, 8 * h_:8 * h_ + 8, :], b[:, 0:512].rearrange("p (s v) -> p s v", v=64),
                             C("rm16", 8 * h_, 8 * h_ + 8).to_broadcast([128, 8, 64]) if False else
                             C("rm16", 8 * h_, 8 * h_ + 8).unsqueeze(2).to_broadcast([128, 8, 64]), ALU.mult)
                    T.op("dve", lambda e: e.tensor_reduce(red, vx[:].rearrange("p s v -> p v s"), AX.X, ALU.add),
                         [vx[:]], [red])

                def sample_state_update(Sb, eb, pairs, pool="big"):
                    T.tt(Sb[:], Sb[:], eb.unsqueeze(2).to_broadcast([128, 16, 64]), ALU.mult, eng="dve")
                    for h_ in range(2):
                        b = bank(pool)
                        for i_, (kp_, x_) in enumerate(pairs):
                            vx = vexp[0]
                            T.tt(vx[:, 0:8, :], x_.unsqueeze(1).to_broadcast([128, 8, 64]),
                                 C("rm16", 8 * h_, 8 * h_ + 8).unsqueeze(2).to_broadcast([128, 8, 64]), ALU.mult, eng="dve")
                            T.matmul(b[:, 0:512], kp_, vx[:, 0:8, :].rearrange("p s v -> p (s v)"),
                                     start=(i_ == 0), stop=(i_ == len(pairs) - 1))
                        T.tt(Sb[:, 8 * h_:8 * h_ + 8, :], Sb[:, 8 * h_:8 * h_ + 8, :],
                             b[:, 0:512].rearrange("p (s v) -> p s v", v=64), ALU.add)

                def state_in(m, pr):
                    if m == "C":
                        return ss_in[m][l].rearrange("s h k v -> (h k) s v")
                    return ss_in[m][l, :, 2 * pr:2 * pr + 2].rearrange("s h k v -> (h k) s v")

                def state_out(m, pr):
                    if m == "C":
                        return ss_out[m][l].rearrange("s h k v -> (h k) s v")
                    return ss_out[m][l, :, 2 * pr:2 * pr + 2].rearrange("s h k v -> (h k) s v")

                def head_norm(oa_t, nst, slot0, nsl, dsl, scratch, bdt_fn, bname):
                    oa = oa_t[:, 0:nch, :, :]
                    if dsl:
                        d0, d1 = dsl[0], dsl[-1] + 1
                        T.op("dve", lambda e: e.tensor_reduce(nst[:, 0:nch, d0:d1, 0], oa_t[:, 0:nch, d0:d1, :], AX.X, ALU.add),
                             [oa_t[:]], [nst[:]])
                        T.ts(nst[:, 0:nch, d0:d1, 0], nst[:, 0:nch, d0:d1, 0], 1.0 / 64.0, ALU.mult)
                        T.tt(oa_t[:, 0:nch, d0:d1, :], oa_t[:, 0:nch, d0:d1, :],
                             nst[:, 0:nch, d0:d1, 0:1].to_broadcast([128, nch, d1 - d0, 64]), ALU.subtract)
                    osq = scratch[:].rearrange("p a b -> p (a b)")[:, 0:nch * 128].rearrange("p (c v) -> p c v", v=64)
                    for half in range(nsl // 2):
                        T.tt(osq[:, 0:nch * 2, :].rearrange("p (c s) v -> p c s v", s=2), oa_t[:, 0:nch, 2 * half:2 * half + 2, :],
                             oa_t[:, 0:nch, 2 * half:2 * half + 2, :], ALU.mult)
                        T.op("dve", lambda e, half=half: e.tensor_reduce(
                            nst[:, 0:nch, 2 * half:2 * half + 2, 1],
                            osq[:, 0:nch * 2, :].rearrange("p (c s) v -> p c s v", s=2), AX.X, ALU.add), [scratch[:]], [nst[:]])
                    yield
                    nr = nsl - len(dsl)
                    T.ts(nst[:, 0:nch, 0:nr, 1], nst[:, 0:nch, 0:nr, 1], 1.0 / 64.0, ALU.mult, NORM_EPS, ALU.add)
                    if dsl:
                        T.ts(nst[:, 0:nch, nr:nsl, 1], nst[:, 0:nch, nr:nsl, 1], 1.0 / 64.0, ALU.mult, LN_EPS, ALU.add)
                    T.act(nst[:, 0:nch, :, 1], nst[:, 0:nch, :, 1], AF.Ln)
                    T.act(nst[:, 0:nch, :, 1], nst[:, 0:nch, :, 1], AF.Exp, scale=-0.5)
                    T.tt(oa, oa, nst[:, 0:nch, :, 1:2].to_broadcast([128, nch, nsl, 64]), ALU.mult)
                    yield
                    for sl_ in range(nsl):
                        slot = slot0 + sl_
                        m, pr = MIX[slot // 2], slot % 2
                        obd = bdt_fn()
                        for blk in range(2):
                            T.ts(obd[:, 0:nch, blk * 64:(blk + 1) * 64], oa_t[:, 0:nch, sl_, :], C("rm64", blk, blk + 1), ALU.mult)
                        pb = bank(bname)
                        for c in range(nch):
                            T.matmul(pb[:, c * 64:(c + 1) * 64], obd[:, c, :], I2)
                        ng_ = (l * 4 + slot // 2) * 2 + pr
                        T.stt(oT[:, slot, 0:nt], pb[:, 0:nt], P("normg", ng_, ng_ + 1), sg[m][pr][:, 0:nt], ALU.mult, ALU.mult)
                        yield

                def sqb(j_):
                    return bdtmp[9 + j_ // 2][:].rearrange("p a b -> p (a b)")[:, (j_ % 2) * 256:(j_ % 2 + 1) * 256]

                def gen_A():
                    for j in range(6):
                        acc = tmpD()

                        def wtap(tap):
                            o_ = (l * 6 + j) * 4 + tap
                            return P("convw", o_, o_ + 1)
                        T.ts(acc[:, 0:nreg], cb[j][:, 0:nreg], wtap(0), ALU.mult, eng="dve")
                        for tap in range(1, 4):
                            T.stt(acc[:, 0:nreg], cb[j][:, tap:tap + nreg], wtap(tap), acc[:, 0:nreg],
                                  ALU.mult, ALU.add, eng="dve")
                        if has_s:
                            av = acc[:, 64:128].rearrange("p (s t) -> p s t", t=4)
                            T.ts(av, cbs[j][:, :, 0:4], wtap(0), ALU.mult, eng="dve")
                            for tap in range(1, 4):
                                T.stt(av, cbs[j][:, :, tap:tap + 4], wtap(tap), av, ALU.mult, ALU.add, eng="dve")
                        T.act(cs[j][:, 0:nt], acc[:, 0:nt], AF.Silu)
                        yield
                        if j < 4:
                            T.tt(sqb(j)[:, 0:nt], cs[j][:, 0:nt], cs[j][:, 0:nt], ALU.mult, eng="dve")
                        if has_s:
                            cst = tmpD()
                            T.copy(cst[:, 0:3], cb[j][:, nreg:nreg + 3], eng="dve")
                            T.copy(cst[:, 3:51].rearrange("p (s j) -> p s j", j=3), cbs[j][:, :, 4:7], eng="dve")
                            bt_ = banks[3]
                            T.transpose(bt_[0:51, 0:128], cst[:, 0:51], ident)
                            T.copy(cstage[0:51, j * 128:(j + 1) * 128], bt_[0:51, 0:128], eng="act")
                            if j == 5:
                                T.dma(convp_d[l], cstage[0:3, 0:768])
                                T.dma(convs_d[l].rearrange("s j f -> (s j) f"), cstage[3:51, 0:768])
                        else:
                            T.copy(hist[l][:, j, :], cb[j][:, nreg:nreg + 3], eng="dve")
                    for j in range(4):
                        sq = tmpD()
                        b = banks[3]
                        T.matmul(b[:, 0:nt], C("bones"), sqb(j)[:, 0:nt])
                        yield
                        T.ts(sq[:, 0:nt], b[:, 0:nt], NORM_EPS, ALU.add)
                        T.act(sq[:, 0:nt], sq[:, 0:nt], AF.Ln)
                        T.act(sq[:, 0:nt], sq[:, 0:nt], AF.Exp, scale=-0.5)
                        if j < 2:
                            T.stt(cs[j][:, 0:nt], cs[j][:, 0:nt], 0.125, sq[:, 0:nt], ALU.mult, ALU.mult, eng="dve")
                        else:
                            T.tt(cs[j][:, 0:nt], cs[j][:, 0:nt], sq[:, 0:nt], ALU.mult, eng="dve")
                    r4 = slice(0, 4)
                    b = banks[3]
                    T.matmul(b[0:4, 0:nt], c8[:, 0:4], ba8[:, 0:nt])
                    T.act(s4g[:, 0:nt], b[0:4, 0:nt], AF.Exp, bias=pp4[:, 2 + l:3 + l])
                    T.ts(s4g[:, 0:nt], s4g[:, 0:nt], 1.0, ALU.add)
                    T.act(s4g[:, 0:nt], s4g[:, 0:nt], AF.Ln)
                    T.ts(s4g[:, 0:nt], s4g[:, 0:nt], nA[:, l:l + 1], ALU.mult)
                    T.act(s4q[:, 2, 0:nt], ba8[0:4, 0:nt], AF.Exp, scale=-1.0)
                    T.ts(s4q[:, 2, 0:nt], s4q[:, 2, 0:nt], 1.0, ALU.add)
                    T.recip(s4q[:, 2, 0:nt], s4q[:, 2, 0:nt])
                    bT4 = s4q[:, 0, :]
                    local_cumsum(s4g, s4cs, bT4, r4, 4)
                    T.ts(s4q[:, 1, 0:nt], s4q[:, 0, 0:nt], -1.0, ALU.mult, eng="dve")
                    T.act(s4q[:, 3, 0:nt], s4q[:, 0, 0:nt], AF.Exp)
                    bl_minus_b(s4q[:, 4, :], bT4, r4, 4)
                    T.act(s4q[:, 4, 0:nt], s4q[:, 4, 0:nt], AF.Exp)
                    exp_bl(ebl4, bT4, r4)
                    yield
                    for pr in range(2):
                        qbd, kbd, vbd = bdq[pr], kTbd[pr], next_bdt()
                        for blk in range(2):
                            rmc = C("rm64", blk, blk + 1)
                            sl = slice(blk * 64, (blk + 1) * 64)
                            v3 = lambda t_: t_[:, 0:nt].rearrange("p (c t) -> p c t", t=64)
                            T.act(qbd[:, 0:nch, sl], v3(cs[pr]), AF.Copy, scale=rmc)
                            T.ts(kbd[:, 0:nch, sl], v3(cs[2 + pr]), rmc, ALU.mult)
                            T.act(vbd[:, 0:nch, sl], v3(cs[4 + pr]), AF.Copy, scale=rmc)
                        pb = bank("prep")
                        for c in range(nch):
                            T.matmul(pb[:, c * 64:(c + 1) * 64], vbd[:, c, :], I2)
                        T.copy(vpair[("A", pr)][:, 0:nch, :], pb[:, 0:nch * 64].rearrange("p (c v) -> p c v", v=64), eng="act")
                        yield
                        P1, P2, P3, P4, P5 = banks[0], banks[1], banks[2], banks[3], banks[4]
                        for (kind, c) in chunks:
                            rx = next_rowx()
                            T.tt(rx[:].rearrange("p q (h t) -> p q h t", t=64),
                                 s4q[:, :, c * 64:(c + 1) * 64].unsqueeze(2).to_broadcast([4, 5, 2, 64]),
                                 hm[:, 2 * pr:2 * pr + 2].unsqueeze(1).unsqueeze(3).to_broadcast([4, 5, 2, 64]),
                                 ALU.mult)
                            Bx, NBx, Betax, Gamx, Wx = (rx[:, i_, :] for i_ in range(5))
                            cc = slice(c * 128, (c + 1) * 128)
                            T.matmul(P5[:, c * 8 + 0:c * 8 + 1], Betax, ones4)
                            T.matmul(P5[:, c * 8 + 1:c * 8 + 2], Gamx, ones4)
                            T.matmul(P5[:, c * 8 + 2:c * 8 + 3], Wx, ones4)
                            if kind == "r":
                                T.matmul(P5[:, c * 8 + 3:c * 8 + 4], hsel(pr), ebl4[:, c:c + 1])
                            else:
                                T.matmul(P5[:, c * 8 + 3:c * 8 + 4], hsel(pr), ebl4[:, 4:5])
                                T.matmul(P5[:, 64:80], hsel(pr), ebl4[:, 4:20])
                            T.matmul(P1[:, cc], hsel(pr), Bx, start=True, stop=False)
                            T.matmul(P1[:, cc], NBx, hsel(pr), start=False, stop=True)
                            T.matmul(P2[:, cc], kbd[:, c, :], kbd[:, c, :])
                            T.matmul(P3[:, cc], kbd[:, c, :], qbd[:, c, :])
                            T.matmul(P4[:, cc], onesrow, Betax)
                            yield
                        T.copy(colsb[:, 0:nch, pr, 0:4], P5[:, 0:nch * 8].rearrange("p (c j) -> p c j", j=8)[:, :, 0:4], eng="act")
                        if has_s:
                            T.copy(eblS[pr][:], P5[:, 64:80], eng="act")
                        T.ts(colsb[:, 0:nch, pr, 4:6], colsb[:, 0:nch, pr, 0:2], -1.0, ALU.mult)
                        v4 = lambda bk: bk[:, 0:nch * 128].rearrange("p (c t) -> p c t", t=128)
                        dT, t1, t2 = next_bdt(), next_bdt(), next_bdt()
                        T.ts(dT[:, 0:nch, :], v4(P1), 0.0, ALU.min)
                        T.act(dT[:, 0:nch, :], dT[:, 0:nch, :], AF.Exp)
                        yield
                        T.tt(t1[:, 0:nchr, :], dT[:, 0:nchr, :], C("mTi").unsqueeze(1).to_broadcast([128, nchr, 128]), ALU.mult)
                        T.tt(t2[:, 0:nchr, :], dT[:, 0:nchr, :], C("mTs").unsqueeze(1).to_broadcast([128, nchr, 128]), ALU.mult)
                        if has_s:
                            T.tt(t1[:, nchr, :], dT[:, nchr, :], C("smTi"), ALU.mult)
                            T.tt(t2[:, nchr, :], dT[:, nchr, :], C("smTs"), ALU.mult)
                        T.tt(attT[pr][:, 0:nch, :], v4(P3), t1[:, 0:nch, :], ALU.mult)
                        T.tt(t2[:, 0:nch, :], v4(P2), t2[:, 0:nch, :], ALU.mult)
                        yield
                        MT = next_bdt()
                        T.stt(MT[:, 0:nch, :], v4(P4), -1.0, t2[:, 0:nch, :], ALU.mult, ALU.mult)
                        for c in range(nch):
                            T.transpose(P1[:, c * 128:(c + 1) * 128], MT[:, c, :], ident)
                        Mm = next_bdt()
                        T.copy(Mm[:, 0:nch, :], v4(P1), eng="act")
                        X = next_bdt()
                        T.tt(X[:, 0:nch, :], MT[:, 0:nch, :], ident.unsqueeze(1).to_broadcast([128, nch, 128]), ALU.add)
                        for j in range(1, 6):
                            Q1, Q2, Q3 = (banks[2], banks[3], banks[4]) if j % 2 else (banks[0], banks[1], banks[4])
                            for c in range(nch):
                                T.matmul(Q1[:, c * 128:(c + 1) * 128], MT[:, c, :], Mm[:, c, :])
                            if j < 5:
                                for c in range(nch):
                                    T.matmul(Q2[:, c * 128:(c + 1) * 128], Mm[:, c, :], MT[:, c, :])
                            Mn = next_bdt()
                            T.copy(Mn[:, 0:nch, :], v4(Q1), eng="act")
                            yield
                            if j < 5:
                                MTn = next_bdt()
                                T.copy(MTn[:, 0:nch, :], v4(Q2), eng="act")
                            for c in range(nch):
                                T.matmul(Q3[:, c * 128:(c + 1) * 128], Mn[:, c, :], X[:, c, :])
                            Xn = next_bdt() if j < 5 else TT[pr]
                            T.tt(Xn[:, 0:nch, :], v4(Q3), X[:, 0:nch, :], ALU.add)
                            yield
                            if j < 5:
                                Mm, MT, X = Mn, MTn, Xn
                        for c in range(nch):
                            T.transpose(P1[:, c * 128:(c + 1) * 128], kbd[:, c, :], ident)
                        T.tt(kpair[pr][:, 0:nch, :], v4(P1), colsb[:, 0:nch, pr, 2:3].to_broadcast([128, nch, 128]), ALU.mult)
                        yield
                    for (kind, c) in chunks:
                        for pr in range(2):
                            slot = pr
                            BD = banks[7] if pr == 0 else banks[4]
                            gamc, ngamc, nbetac, eblc_ = (colsb[:, c, pr, 1:2], colsb[:, c, pr, 5:6],
                                                          colsb[:, c, pr, 4:5], colsb[:, c, pr, 3:4])
                            tr = next_sm()
                            if kind == "r":
                                S = Sp[(l, "A", pr)]
                                T.matmul(BD[:, 0:64], kTbd[pr][:, c, :], S[:])
                                T.stt(tr[:, 0:64], BD[:, 0:64], gamc, vpair[("A", pr)][:, c, :], ALU.mult, ALU.subtract)
                            else:
                                Sb = Sspool[0]
                                T.dma(Sb[:], state_in("A", pr))
                                red = next_sm()
                                sample_inter(kTbd[pr][:, c, :], Sb, red[:, 0:64])
                                T.stt(tr[:, 0:64], red[:, 0:64], gamc, vpair[("A", pr)][:, c, :], ALU.mult, ALU.subtract)
                            T.ts(tr[:, 0:64], tr[:, 0:64], nbetac, ALU.mult)
                            yield
                            T.matmul(BD[:, 64:128], TT[pr][:, c, :], tr[:, 0:64])
                            u = next_sm()
                            T.copy(u[:, 0:64], BD[:, 64:128], eng="act")
                            yield
                            T.matmul(BD[:, 128:192], attT[pr][:, c, :], u[:, 0:64])
                            o1 = next_sm()
                            T.copy(o1[:, 0:64], BD[:, 128:192], eng="act")
                            if kind == "r":
                                T.matmul(BD[:, 192:256], bdq[pr][:, c, :], S[:])
                                T.matmul(BD[:, 256:320], kpair[pr][:, c, :], u[:, 0:64])
                                T.stt(o_allA[:, c, slot, :], BD[:, 192:256], gamc, o1[:, 0:64], ALU.mult, ALU.add)
                                T.stt(S[:], S[:], eblc_, BD[:, 256:320], ALU.mult, ALU.add)
                                yield
                            else:
                                red2 = next_sm()
                                sample_inter(bdq[pr][:, c, :], Sb, red2[:, 0:64])
                                T.stt(o_allA[:, c, slot, :], red2[:, 0:64], gamc, o1[:, 0:64], ALU.mult, ALU.add)
                                sample_state_update(Sb, eblS[pr][:], [(kpair[pr][:, c, :], u[:, 0:64])])
                                T.dma(state_out("A", pr), Sb[:])
                    if has_s:
                        for pr in range(2):
                            T.dma(sp_out["A"][l, 2 * pr:2 * pr + 2].rearrange("h k v -> (h k) v"), Sp[(l, "A", pr)][:])

                    yield
                def gen_G():
                    for m in ("B", "C", "D"):
                        mi = MIX.index(m)
                        for pr in range(2):
                            if m == "C":
                                qF, kF, gF = qk[("C", "q")][0], qk[("C", "k")][0], qk[("C", "g")][0]
                                rmn, rmi = "rm32", (2 * pr, 2 * pr + 1)
                            else:
                                qF, kF = qk[(m, "q")][pr], qk[(m, "k")][pr]
                                rmn, rmi = "rm64", (0, 1)
                                if m == "B":
                                    gF = qk[("B", "g")][pr]
                                else:
                                    gF = tmpG()
                                    T.memset(gF[:, 0:nt], 1.0)
                                    T.ts(gF[:, 0:nt], gF[:, 0:nt], C("lg", pr, pr + 1), ALU.mult, eng="dve")
                            if not (m == "C" and pr == 1):
                                csb, bT = tmpG(), tmpG()
                                local_cumsum(gF, csb, bT)
                                e1, e2, d3 = tmpG(), tmpG(), tmpG()
                                T.act(e1[:, 0:nt], bT[:, 0:nt], AF.Exp)
                                T.act(e2[:, 0:nt], bT[:, 0:nt], AF.Exp, scale=-1.0)
                                bl_minus_b(d3, bT)
                                T.act(d3[:, 0:nt], d3[:, 0:nt], AF.Exp)
                            exp_bl(eblc[pr], bT)
                            yield
                            qbd, kbd, khbd = bdqG[pr], next_bdtG(), next_bdtG()
                            v3 = lambda t_: t_[:, 0:nt].rearrange("p (c t) -> p c t", t=64)
                            for blk in range(2):
                                rmc = C(rmn, rmi[blk], rmi[blk] + 1)
                                sl = slice(blk * 64, (blk + 1) * 64)
                                T.stt(qbd[:, 0:nch, sl], v3(qF), rmc, v3(e1), ALU.mult, ALU.mult, eng="dve")
                                T.stt(kbd[:, 0:nch, sl], v3(kF), rmc, v3(e2), ALU.mult, ALU.mult, eng="dve")
                                T.stt(khbd[:, 0:nch, sl], v3(kF), rmc, v3(d3), ALU.mult, ALU.mult, eng="dve")
                                yield
                            pb = bank("G")
                            for c in range(nch):
                                T.matmul(pb[:, c * 128:(c + 1) * 128], kbd[:, c, :], qbd[:, c, :])
                            T.tt(attTG[pr][:, 0:nchr, :], pb[:, 0:nchr * 128].rearrange("p (c t) -> p c t", t=128),
                                 C("mTi").unsqueeze(1).to_broadcast([128, nchr, 128]), ALU.mult)
                            if has_s:
                                T.tt(attTG[pr][:, nchr, :], pb[:, nchr * 128:(nchr + 1) * 128], C("smTi"), ALU.mult)
                            pb = bank("G")
                            for c in range(nch):
                                T.transpose(pb[:, c * 128:(c + 1) * 128], khbd[:, c, :], ident)
                            T.copy(kpairG[pr][:, 0:nch, :], pb[:, 0:nch * 128].rearrange("p (c k) -> p c k", k=128), eng="act")
                            yield
                        for pr in range(2):
                            slot = mi * 2 + pr
                            S = Sp[(l, m, pr)]
                            psS = banks[5][:, pr * 256:(pr + 1) * 256]
                            for c in range(nchr):
                                if m == "C":
                                    T.matmul(psS[:, c * 64:(c + 1) * 64], kpairG[0][:, c, :], vpair[(m, 0)][:, c, :], start=True, stop=False)
                                    T.matmul(psS[:, c * 64:(c + 1) * 64], kpairG[1][:, c, :], vpair[(m, 1)][:, c, :], start=False, stop=True)
                                else:
                                    T.matmul(psS[:, c * 64:(c + 1) * 64], kpairG[pr][:, c, :], vpair[(m, pr)][:, c, :])
                            prev = S[:]
                            for c in range(nchr):
                                dst = Sch[pr][:, c, :]
                                T.stt(dst, prev, eblc[pr][:, c:c + 1], psS[:, c * 64:(c + 1) * 64], ALU.mult, ALU.add)
                                prev = dst
                                yield
                            po = banks[6][:, pr * 256:(pr + 1) * 256]
                            for c in range(nchr):
                                sprev = S[:] if c == 0 else Sch[pr][:, c - 1, :]
                                T.matmul(po[:, c * 64:(c + 1) * 64], attTG[pr][:, c, :], vpair[(m, pr)][:, c, :], start=True, stop=False)
                                T.matmul(po[:, c * 64:(c + 1) * 64], bdqG[pr][:, c, :], sprev, start=False, stop=True)
                            T.copy(o_allG[:, 0:nchr, slot - 2, :], po[:, 0:nchr * 64].rearrange("p (c v) -> p c v", v=64), eng="act")
                            T.copy(S[:], Sch[pr][:, nchr - 1, :], eng="dve")
                            yield
                        for (kind, c) in chunks:
                            if kind == "r":
                                continue
                            else:
                                for pr in range(2):
                                    slot = mi * 2 + pr
                                    if not (m == "C" and pr == 1):
                                        Sb = Sspool[1]
                                        T.dma(Sb[:], state_in(m, pr))
                                    red = redG
                                    sample_inter(bdqG[pr][:, c, :], Sb, red[:, 0:64], pool="G")
                                    po = banks[6][:, pr * 64:(pr + 1) * 64]
                                    T.matmul(po, attTG[pr][:, c, :], vpair[(m, pr)][:, c, :])
                                    T.tt(o_allG[:, c, slot - 2, :], po, red[:, 0:64], ALU.add)
                                    yield
                                    if m != "C":
                                        sample_state_update(Sb, eblc[pr][:, 4:20], [(kpairG[pr][:, c, :], vpair[(m, pr)][:, c, :])], pool="G")
                                        T.dma(state_out(m, pr), Sb[:])
                                    elif pr == 1:
                                        sample_state_update(Sb, eblc[pr][:, 4:20],
                                                            [(kpairG[0][:, c, :], vpair[(m, 0)][:, c, :]),
                                                             (kpairG[1][:, c, :], vpair[(m, 1)][:, c, :])], pool="G")
                                        T.dma(state_out(m, pr), Sb[:])
                        if has_s:
                            if m == "C":
                                T.dma(sp_out[m][l].rearrange("h k v -> (h k) v"), Sp[(l, m, 0)][:])
                            else:
                                for pr in range(2):
                                    T.dma(sp_out[m][l, 2 * pr:2 * pr + 2].rearrange("h k v -> (h k) v"), Sp[(l, m, pr)][:])

                    yield
                    for _ in head_norm(o_allG, nstatG, 2, 6, (4, 5), bdtmp[9], next_bdtG, "G"):
                        yield
                gA = gen_A()
                n_chunk, n_front = [0], [0]
                deferred = []
                for (c0, c1, chs) in SLABS:
                    w = c1 - c0
                    sv = load_slab("in", l, c0, 8, w)
                    for (name, cc, cw) in chs:
                        if n_chunk[0] >= 7 and (n_chunk[0] - 7) % 2 == 0 and n_front[0] < 11:
                            next(gA)
                            n_front[0] += 1
                        n_chunk[0] += 1
                        b = bank("st6")
                        for kc in range(8):
                            T.matmul(b[0:cw, 0:nt], sv[:, kc, cc - c0:cc - c0 + cw], xTb[:, kc, 0:nt],
                                     start=(kc == 0), stop=(kc == 7))
                        ps = b[0:cw, 0:nt]
                        kind = name[:2]
                        idx = int(name[2]) if len(name) == 3 and name[2].isdigit() else 0
                        if kind in ("aq", "ak", "av"):
                            j = {"aq": 0, "ak": 2, "av": 4}[kind] + idx
                            T.copy(cb[j][:, 3:3 + nreg], b[:, 0:nreg], eng="act")
                            if has_s:
                                T.copy(cbs[j][:, :, 3:7], b[:, 64:128].rearrange("p (s t) -> p s t", t=4), eng="dve")
                        elif name == "aba":
                            T.copy(ba8[:, 0:nt], ps, eng="dve")
                        elif kind in ("ag", "bg", "cg", "dg"):
                            T.act(sg[name[0].upper()][idx][:, 0:nt], ps, AF.Silu)
                        elif kind == "bq":
                            T.act(qk[("B", "q")][idx][:, 0:nt], ps, AF.Silu)
                        elif kind == "bf":
                            e = tmpG()
                            T.act(e[:, 0:nt], ps, AF.Exp)
                            rl = rlb[:, 2 * l + idx: 2 * l + idx + 1]
                            T.ts(e[:, 0:nt], e[:, 0:nt], rl, ALU.mult, rl, ALU.add)
                            kk_ = qk[("B", "k")][idx]
                            T.recip(kk_[:, 0:nt], e[:, 0:nt])
                            T.ts(e[:, 0:nt], kk_[:, 0:nt], 1.0 - 1e-6, ALU.min, -1.0, ALU.mult, eng="dve")
                            T.act(qk[("B", "g")][idx][:, 0:nt], e[:, 0:nt], AF.Ln, bias=1.0)
                        elif kind in ("bi", "cv", "dv"):
                            m = name[0].upper()
                            pr = idx
                            vb = next_bdt()
                            nch = nt // 64
                            for blk in range(2):
                                if blk == 0:
                                    T.act(vb[:, 0:nch, 0:64], b[:, 0:nt].rearrange("p (c t) -> p c t", t=64),
                                          AF.Copy, scale=C("rm64", 0, 1))
                                else:
                                    T.ts(vb[:, 0:nch, 64:128], b[:, 0:nt].rearrange("p (c t) -> p c t", t=64),
                                         C("rm64", 1, 2), ALU.mult)

                            def _fin_v(vb=vb, m=m, pr=pr, nch=nch):
                                pb = bank("prep")
                                for c in range(nch):
                                    T.matmul(pb[:, c * 64:(c + 1) * 64], vb[:, c, :], I2)
                                T.copy(vpair[(m, pr)][:, 0:nch, :], pb[:, 0:nch * 64].rearrange("p (c v) -> p c v", v=64), eng="act")
                            deferred.append(_fin_v)
                        elif name == "cq":
                            T.act(qk[("C", "q")][0][:, 0:nt], ps, AF.Copy, scale=32.0 ** -0.5)
                        elif name == "ck":
                            T.copy(qk[("C", "k")][0][:, 0:nt], ps, eng="dve")
                        elif name == "clr":
                            T.copy(lrT[:, 0:nt], ps, eng="dve")

                            def _fin_clr():
                                b2 = bank("st6")
                                T.matmul(b2[:, 0:nt], wgc[:, l * 128:(l + 1) * 128], lrT[:, 0:nt])
                                e = tmpG()
                                T.act(e[:, 0:nt], b2[:, 0:nt], AF.Exp, bias=negbg[:, l:l + 1], scale=-1.0)
                                T.act(e[:, 0:nt], e[:, 0:nt], AF.Ln, bias=1.0)
                                T.ts(qk[("C", "g")][0][:, 0:nt], e[:, 0:nt], -1.0 / 16.0, ALU.mult, eng="dve")
                            deferred.append(_fin_clr)
                        elif kind in ("dq", "dk"):
                            raw = vexp[0][:].rearrange("p a b -> p (a b)")[:, ((0 if kind == "dq" else 2) + idx) * 256:((0 if kind == "dq" else 2) + idx + 1) * 256]
                            if kind == "dq":
                                T.copy(raw[:, 0:nt], ps, eng="act")
                            else:
                                T.act(raw[:, 0:nt], ps, AF.Copy, scale=0.125)

                            def _fin_d(raw=raw, kind=kind, idx=idx):
                                b2 = bank("st6")
                                T.matmul(b2[:, 0:nt], C("rot"), raw[:, 0:nt])
                                t1 = tmpG()
                                T.tt(t1[:, 0:nt], raw[:, 0:nt], cosb[:, 0:nt], ALU.mult, eng="dve")
                                t2 = tmpG()
                                T.tt(t2[:, 0:nt], b2[:, 0:nt], sinb[:, 0:nt], ALU.mult)
                                T.tt(qk[("D", kind[1])][idx][:, 0:nt], t1[:, 0:nt], t2[:, 0:nt], ALU.add, eng="dve")
                            deferred.append(_fin_d)
                        else:
                            raise AssertionError(name)
                while n_front[0] < 11:
                    next(gA)
                    n_front[0] += 1
                for fn_ in deferred:
                    fn_()

                if dbg == "proj" and l == 0 and ti in (0, 8):
                    o_ = 0 if ti == 0 else 2048
                    T.dma(dbg_d[:, o_ + 0:o_ + nt], qk[("B", "q")][0][:, 0:nt])
                    T.dma(dbg_d[:, o_ + 256:o_ + 256 + nt], qk[("B", "k")][1][:, 0:nt])
                    T.dma(dbg_d[:, o_ + 512:o_ + 512 + nt], qk[("B", "g")][1][:, 0:nt])
                    T.dma(dbg_d[:, o_ + 768:o_ + 768 + nt], qk[("C", "g")][0][:, 0:nt])
                    T.dma(dbg_d[:, o_ + 1024:o_ + 1024 + nt], qk[("D", "q")][1][:, 0:nt])
                    T.dma(dbg_d[:, o_ + 1280:o_ + 1280 + nt], qk[("D", "k")][0][:, 0:nt])
                    T.dma(dbg_d[:, o_ + 1536:o_ + 1536 + 256], vpair[("B", 1)][:].rearrange("p c v -> p (c v)"))
                    T.dma(dbg_d[:, o_ + 1792:o_ + 1792 + nt], xT[:, 3, 0:nt])

                _gens = [gA, gen_G()]
                while _gens:
                    for g_ in list(_gens):
                        try:
                            next(g_)
                        except StopIteration:
                            _gens.remove(g_)

                if l == nlayers - 1 and ti + 1 < NTILES:
                    nt_next = NT if ti + 1 < 8 else 128
                    for blk_ in range(nt_next // 128):
                        prefetched[(ti + 1, blk_)] = load_block(ti + 1, blk_)

                svs, accs = [], []
                for q4 in range(3):
                    sv = load_slab("out", l, q4 * 256, 8, 256)
                    svs.append(sv)
                    for mc in range(2):
                        g_ = q4 * 2 + mc
                        acc = banks[(0, 1, 2, 5, 6, 7)[g_]][:, 0:nt]
                        accs.append(acc)
                        for kc in range(2, 8):
                            T.matmul(acc, sv[:, kc, mc * 128:(mc + 1) * 128], oT[:, kc, 0:nt],
                                     start=(kc == 2), stop=False)
                for _ in head_norm(o_allA, nstatA, 0, 2, (), bdtmp[0], next_bdt, "prep"):
                    pass
                for q4 in range(3):
                    for mc in range(2):
                        g_ = q4 * 2 + mc
                        for kc in range(2):
                            T.matmul(accs[g_], svs[q4][:, kc, mc * 128:(mc + 1) * 128], oT[:, kc, 0:nt],
                                     start=False, stop=(kc == 1))
                        T.stt(x1T[:, g_, 0:nt], xT[:, g_, 0:nt], ALPHA, accs[g_], ALU.mult, ALU.add)
                for q4 in range(3, 4):
                    sv = load_slab("out", l, q4 * 256, 8, 256)
                    for mc in range(2):
                        b = bank("big")
                        for kc in range(8):
                            T.matmul(b[:, 0:nt], sv[:, kc, mc * 128:(mc + 1) * 128], oT[:, kc, 0:nt],
                                     start=(kc == 0), stop=(kc == 7))
                        T.stt(x1T[:, q4 * 2 + mc, 0:nt], xT[:, q4 * 2 + mc, 0:nt], ALPHA, b[:, 0:nt], ALU.mult, ALU.add)
                layer_norm_fm(x1T, xT, x1Tb, "ln1g", "ln1b", l * 8, nt)

                for sl_ in range(11):
                    svg = load_slab("g", l, sl_ * 256, 8, 256)
                    svu = load_slab("u", l, sl_ * 256, 8, 256)
                    for mc in range(2):
                        bg, bu = bank("big"), bank("big")
                        for kc in range(8):
                            T.matmul(bg[:, 0:nt], svg[:, kc, mc * 128:(mc + 1) * 128], x1Tb[:, kc, 0:nt],
                                     start=(kc == 0), stop=(kc == 7))
                        for kc in range(8):
                            T.matmul(bu[:, 0:nt], svu[:, kc, mc * 128:(mc + 1) * 128], x1Tb[:, kc, 0:nt],
                                     start=(kc == 0), stop=(kc == 7))
                        sgt = tmp()
                        T.act(sgt[:, 0:nt], bg[:, 0:nt], AF.Silu)
                        T.tt(hT[sl_ * 2 + mc][:, 0:nt], bu[:, 0:nt], sgt[:, 0:nt], ALU.mult)
                for mc in range(8):
                    sva = load_slab("d", l, mc * 128, 11, 128, k0=0)
                    svb = load_slab("d", l, mc * 128, 11, 128, k0=11)
                    b = bank("big")
                    for kc in range(22):
                        sv_ = sva if kc < 11 else svb
                        T.matmul(b[:, 0:nt], sv_[:, kc % 11, :], hT[kc][:, 0:nt], start=(kc == 0), stop=(kc == 21))
                    T.stt(x1T[:, mc, 0:nt], xT[:, mc, 0:nt], ALPHA, b[:, 0:nt], ALU.mult, ALU.add)
                layer_norm_fm(x1T, xT, xTb, "ln2g", "ln2b", l * 8, nt)
                if dbg == "x2" and l == 0 and ti in (0, 8):
                    o_ = 0 if ti == 0 else 2048
                    for kc in range(8):
                        T.dma(dbg_d[:, o_ + kc * 256:o_ + kc * 256 + nt], xT[:, kc, 0:nt])

            for blk in range(nt // 128):
                xo = out_stage[blk]
                for kc in range(8):
                    b = bank("big")
                    T.transpose(b[:, 0:128], xT[:, kc, blk * 128:(blk + 1) * 128], ident)
                    T.copy(xo[:, kc * 128:(kc + 1) * 128], b[:, 0:128], eng=("act" if kc % 2 else "dve"))
                c_lo = col0 + blk * 128
                if ti == 0 and blk == 0:
                    T.dma(yp_d[0:64, :], xo[64:128, :])
                elif ti == 8:
                    T.dma(yp_d[SEQ - 64:SEQ, :], xo[0:64, :])
                    T.dma(ys_d[:, :], xo[64:128, :])
                else:
                    T.dma(yp_d[c_lo - 64:c_lo + 64, :], xo[:, :])
        T.finish()
        print(f"[build] instructions={T.n_ins} waits={T.n_wait}")
    return nc


def make_in_maps(inputs):
    consts = make_consts()
    pp128, pp4, pp16 = pack_params(inputs)
    f = lambda a: np.ascontiguousarray(np.asarray(a, np.float32))
    shared = dict(meta=f(inputs["meta_tokens"]), w_in=f(inputs["w_in"]), w_out=f(inputs["w_out"]),
                  w_g=f(inputs["w_ffn_gate"]), w_u=f(inputs["w_ffn_up"]), w_d=f(inputs["w_ffn_down"]),
                  pp128=pp128, pp4=pp4, pp16=pp16, **consts)
    maps = []
    for c in range(8):
        s = slice(16 * c, 16 * c + 16)
        m = dict(shared)
        m["xp"] = f(inputs["x_prompt"][c])
        m["xs"] = f(inputs["x_sample"][s]).reshape(NSAMP, D)
        m["sconv"] = f(inputs["state_delta_conv"][:, s]).reshape(NL, 48, 768)
        m["sdelta"] = f(inputs["state_delta"][:, s])
        m["shgrn"] = f(inputs["state_hgrn"][:, s])
        m["sgla"] = f(inputs["state_gla"][:, s])
        m["sret"] = f(inputs["state_ret"][:, s])
        maps.append(m)
    return maps


def kernel(**inputs):
    nc = build_program()
    maps = make_in_maps(inputs)
    res = run_bass_kernel_spmd(nc, maps, core_ids=list(range(8)))
    r = res.results
    cat = lambda k: np.stack([np.asarray(r[c][k]) for c in range(8)], axis=0)
    y_p = cat("y_p")
    y_s = cat("y_s").reshape(128, 4, D)
    conv_p = cat("conv_p").transpose(1, 0, 2, 3)
    conv_s = np.concatenate([np.asarray(r[c]["conv_s"]) for c in range(8)], axis=1)
    outs = [y_p, y_s, conv_p, conv_s]
    for nm in ("delta", "hgrn", "gla", "ret"):
        outs.append(cat(nm + "_p").transpose(1, 0, 2, 3, 4))
        outs.append(np.concatenate([np.asarray(r[c][nm + "_s"]) for c in range(8)], axis=1))
    return tuple(np.ascontiguousarray(o, dtype=np.float32) for o in outs)
```

```python
import numpy as np
from contextlib import ExitStack
import concourse.bass as bass
import concourse.mybir as mybir
from concourse.bass_utils import run_bass_kernel_spmd

F32 = mybir.dt.float32
BF16 = mybir.dt.bfloat16
AF = mybir.ActivationFunctionType
ALU = mybir.AluOpType
AX = mybir.AxisListType

D = 1024
DIN = 3864
DFF = 2816
NL = 2
SEQ = 2048
NPAD = 48
NMETA = 16
NPROMPT = NPAD + NMETA + SEQ
NSAMP = 64
NTOK = NPROMPT + NSAMP
NT = 256
NTILES = 9
ALPHA = (2 * NL) ** 0.25
LN_EPS = 1e-5
NORM_EPS = 1e-6
PAST_LEN = 16384
N_DMA_SEMS = 16


class Tracker:
    def __init__(self, nc, stack):
        self.nc = nc
        self.eng = {"pe": nc.tensor, "act": nc.scalar, "dve": nc.vector,
                    "pool": nc.gpsimd, "sp": nc.sync}
        self.sem = {k: stack.enter_context(nc.semaphore("s_" + k)) for k in self.eng}
        self.cnt = {k: 0 for k in self.eng}
        self.dsem = {q: [stack.enter_context(nc.semaphore(f"d_{q}{i}")) for i in range(N_DMA_SEMS)]
                     for q in ("sp", "pool")}
        self.dcnt = {q: [0] * N_DMA_SEMS for q in self.dsem}
        self.dnext = {q: 0 for q in self.dsem}
        self.seen = {}
        self.lastw = {}
        self.readers = {}
        self.n_wait = 0
        self.n_ins = 0

    @staticmethod
    def _key(x):
        if isinstance(x, tuple):
            return (x[0].name, x[1])
        return (x.name, None)

    def _wait(self, e, tok):
        sem, val, src = tok
        k = (e, sem.name)
        if self.seen.get(k, 0) >= val:
            return
        self.seen[k] = val
        self.eng[e].wait_ge(sem, val)
        self.n_wait += 1

    def _deps(self, e, rk, wk, prk=()):
        for k in rk:
            t = self.lastw.get(k)
            if t is not None:
                self._wait(e, t)
        for k in prk:
            t = self.lastw.get(k)
            if t is not None:
                self._wait(e, t)
            for r in self.readers.get(k, ()):
                if r[2] != e:
                    self._wait(e, r)
        for k in wk:
            t = self.lastw.get(k)
            if t is not None and not (t[2] == e and e == "pe"):
                self._wait(e, t)
            for r in self.readers.get(k, ()):
                self._wait(e, r)

    def _commit(self, tok, rk, wk):
        for k in wk:
            self.lastw[k] = tok
            self.readers[k] = []
        for k in rk:
            if k not in wk:
                self.readers.setdefault(k, []).append(tok)

    def op(self, e, fn, reads, writes):
        rk, prk = [], []
        wk = [self._key(x) for x in writes]
        for x in reads:
            k = self._key(x)
            if k in wk:
                continue
            (prk if k[0].startswith("ps") else rk).append(k)
        self._deps(e, rk, wk, prk)
        ins = fn(self.eng[e])
        self.cnt[e] += 1
        ins.then_inc(self.sem[e], 1)
        self._commit((self.sem[e], self.cnt[e], e), rk + prk, wk)
        self.n_ins += 1
        return ins

    def dma(self, out, in_, q="sp", **kw):
        ok = self._key(out)
        ik = self._key(in_)
        oap = out[0] if isinstance(out, tuple) else out
        iap = in_[0] if isinstance(in_, tuple) else in_
        i = self.dnext[q]
        self.dnext[q] = (i + 1) % N_DMA_SEMS
        sem = self.dsem[q][i]
        if self.dcnt[q][i] > 0:
            self._wait(q, (sem, self.dcnt[q][i], None))
        self._deps(q, [ik], [ok])
        ins = self.eng[q].dma_start(out=oap, in_=iap, **kw)
        self.dcnt[q][i] += 16
        ins.then_inc(sem, 16)
        self._commit((sem, self.dcnt[q][i], None), [ik], [ok])
        self.n_ins += 1
        return ins

    def finish(self):
        for q in self.dsem:
            for i, sem in enumerate(self.dsem[q]):
                if self.dcnt[q][i] > 0:
                    self._wait("sp", (sem, self.dcnt[q][i], None))
        for e in self.eng:
            if e != "sp" and self.cnt[e] > 0:
                self._wait("sp", (self.sem[e], self.cnt[e], e))

    def matmul(self, out, lhsT, rhs, start=True, stop=True):
        return self.op("pe", lambda e: e.matmul(out, lhsT, rhs, start=start, stop=stop),
                       [lhsT, rhs], [out])

    def transpose(self, out, in_, ident):
        return self.op("pe", lambda e: e.transpose(out, in_, ident), [in_, ident], [out])

    def act(self, out, in_, func, bias=None, scale=None):
        kw = {}
        rd = [in_]
        if bias is not None:
            kw["bias"] = bias
            if not isinstance(bias, (int, float)):
                rd.append(bias)
        if scale is not None:
            kw["scale"] = scale
            if not isinstance(scale, (int, float)):
                rd.append(scale)
        return self.op("act", lambda e: e.activation(out, in_, func, **kw), rd, [out])

    def tt(self, out, in0, in1, op, eng="dve"):
        return self.op(eng, lambda e: e.tensor_tensor(out, in0, in1, op), [in0, in1], [out])

    def ts(self, out, in0, s1, op0, s2=None, op1=None, eng="dve"):
        rd = [in0]
        if not isinstance(s1, (int, float)):
            rd.append(s1)
        if s2 is not None and not isinstance(s2, (int, float)):
            rd.append(s2)
        if op1 is None:
            return self.op(eng, lambda e: e.tensor_scalar(out, in0, s1, None, op0), rd, [out])
        return self.op(eng, lambda e: e.tensor_scalar(out, in0, s1, s2, op0, op1), rd, [out])

    def stt(self, out, in0, scalar, in1, op0, op1, eng="dve"):
        rd = [in0, in1]
        if not isinstance(scalar, (int, float)):
            rd.append(scalar)
        return self.op("dve", lambda e: e.scalar_tensor_tensor(out, in0, scalar, in1, op0, op1), rd, [out])

    def copy(self, out, in_, eng="dve"):
        if eng == "act":
            return self.op(eng, lambda e: e.copy(out, in_), [in_], [out])
        return self.op(eng, lambda e: e.tensor_copy(out, in_), [in_], [out])

    def memset(self, out, val, eng="dve"):
        return self.op(eng, lambda e: e.memset(out, val), [], [out])

    def scan(self, out, d0, eng="dve"):
        return self.op(eng, lambda e: e.tensor_tensor_scan(out, d0, d0, 0.0, ALU.add, ALU.bypass),
                       [d0], [out])

    def recip(self, out, in_):
        return self.op("dve", lambda e: e.reciprocal(out, in_), [in_], [out])


C128_LAYOUT = [("ident", 128), ("I2", 64), ("onesM", 128), ("bones", 128), ("rot", 128),
               ("mTi", 128), ("mTs", 128), ("smTi", 128), ("smTs", 128),
               ("rm64", 2), ("rm32", 4), ("rm16", 16), ("lg", 2), ("one", 1)]


def _c128_offsets():
    off = {}
    o = 0
    for n, w in C128_LAYOUT:
        off[n] = (o, w)
        o += w
    return off, o


def make_consts():
    off, tot = _c128_offsets()
    c = np.zeros((128, tot), np.float32)

    def put(name, arr):
        o, w = off[name]
        c[:, o:o + w] = arr
    p = np.arange(128)
    put("ident", np.eye(128, dtype=np.float32))
    put("I2", (p[:, None] % 64 == np.arange(64)[None, :]).astype(np.float32))
    put("onesM", np.full((128, 128), 1.0 / D, np.float32))
    put("bones", (p[:, None] // 64 == p[None, :] // 64).astype(np.float32))
    rot = np.zeros((128, 128), np.float32)
    for i in range(128):
        d = i % 64
        j = i + 32 if d < 32 else i - 32
        rot[i, j] = 1.0
    put("rot", rot)
    hl = p // 64
    s = p % 64
    same = hl[:, None] == hl[None, :]
    put("mTi", (same & (s[:, None] <= s[None, :])).astype(np.float32))
    put("mTs", (same & (s[:, None] < s[None, :])).astype(np.float32))
    sq = s // 4
    tau = s % 4
    sames = same & (sq[:, None] == sq[None, :])
    put("smTi", (sames & (tau[:, None] <= tau[None, :])).astype(np.float32))
    put("smTs", (sames & (tau[:, None] < tau[None, :])).astype(np.float32))
    put("rm64", (p[:, None] // 64 == np.arange(2)[None, :]).astype(np.float32))
    put("rm32", (p[:, None] // 32 == np.arange(4)[None, :]).astype(np.float32))
    put("rm16", (sq[:, None] == np.arange(16)[None, :]).astype(np.float32))
    lgam = np.log1p(-np.exp2(-5.0 - np.arange(4, dtype=np.float32))).astype(np.float32)
    lg = np.zeros((128, 2), np.float32)
    for pr in range(2):
        lg[:, pr] = lgam[2 * pr + hl]
    put("lg", lg)
    put("one", np.ones((128, 1), np.float32))
    c4 = np.zeros((4, 4 + 256 + 128 + 1), np.float32)
    c4[:, 0:4] = np.eye(4)
    for h in range(4):
        pr, hl_ = h // 2, h % 2
        c4[h, 4 + pr * 128 + hl_ * 64: 4 + pr * 128 + hl_ * 64 + 64] = 1.0
    c4[:, 260:388] = 1.0
    c4[:, 388] = 1.0
    c8 = np.zeros((8, 4), np.float32)
    for j in range(4):
        c8[4 + j, j] = 1.0
    pos = np.zeros(NTOK, np.float32)
    pos[:NPROMPT] = np.maximum(np.arange(NPROMPT) - NPAD, 0).astype(np.float32)
    pos[NPROMPT:] = (PAST_LEN + (np.arange(NSAMP) % 4)).astype(np.float32)
    inv = (1.0 / (np.float32(10000.0) ** np.linspace(0.0, 1.0, 32, dtype=np.float32))).astype(np.float32)
    ang = (pos[:, None] * inv[None, :]).astype(np.float32)
    cos = np.cos(ang).astype(np.float32)
    sin = np.sin(ang).astype(np.float32)
    d = p % 64
    cosT = cos[:, d % 32].T.copy()
    sinT = (sin[:, d % 32].T * np.where(d < 32, -1.0, 1.0)[:, None]).astype(np.float32)
    return dict(c128=c, c4=c4, c8=c8, cosT=np.ascontiguousarray(cosT),
                sinT=np.ascontiguousarray(sinT))


SLABS = [
    (0, 256, [("aq0", 0, 128), ("aq1", 128, 128)]),
    (256, 512, [("ak0", 256, 128), ("ak1", 384, 128)]),
    (512, 768, [("av0", 512, 128), ("av1", 640, 128)]),
    (768, 904, [("aba", 768, 8), ("ag0", 776, 128)]),
    (904, 1160, [("ag1", 904, 128), ("bq0", 1032, 128)]),
    (1160, 1416, [("bq1", 1160, 128), ("bf0", 1288, 128)]),
    (1416, 1672, [("bf1", 1416, 128), ("bi0", 1544, 128)]),
    (1672, 1928, [("bi1", 1672, 128), ("bg0", 1800, 128)]),
    (1928, 2184, [("bg1", 1928, 128), ("cq", 2056, 128)]),
    (2184, 2440, [("ck", 2184, 128), ("cv0", 2312, 128)]),
    (2440, 2584, [("cv1", 2440, 128), ("clr", 2568, 16)]),
    (2584, 2840, [("cg0", 2584, 128), ("cg1", 2712, 128)]),
    (2840, 3096, [("dq0", 2840, 128), ("dq1", 2968, 128)]),
    (3096, 3352, [("dk0", 3096, 128), ("dk1", 3224, 128)]),
    (3352, 3608, [("dv0", 3352, 128), ("dv1", 3480, 128)]),
    (3608, 3864, [("dg0", 3608, 128), ("dg1", 3736, 128)]),
]

PP128 = [("embg", 8), ("embb", 8), ("ln1g", 16), ("ln1b", 16), ("ln2g", 16), ("ln2b", 16),
         ("convw", 48), ("normg", 16), ("lbl", 4), ("bgc", 2)]


def _pp_offsets():
    off = {}
    o = 0
    for n, w in PP128:
        off[n] = (o, w)
        o += w
    return off, o


def pack_params(inp):
    off, tot = _pp_offsets()
    a = np.zeros((128, tot), np.float32)

    def put(name, arr):
        o, w = off[name]
        a[:, o:o + w] = np.asarray(arr, np.float32).reshape(128, w)
    fm = lambda v: np.asarray(v, np.float32).reshape(8, 128).T
    put("embg", fm(inp["emb_ln_g"]))
    put("embb", fm(inp["emb_ln_b"]))
    for nm, key in (("ln1g", "ln1_g"), ("ln1b", "ln1_b"), ("ln2g", "ln2_g"), ("ln2b", "ln2_b")):
        put(nm, np.stack([fm(inp[key][l]) for l in range(NL)], axis=1))
    cw = np.asarray(inp["conv_w"], np.float32).reshape(NL, 4, 6, 128)
    put("convw", cw.transpose(3, 0, 2, 1))
    ng = np.stack([np.asarray(inp[k], np.float32).reshape(NL, 2, 128)
                   for k in ("delta_norm_g", "hgrn_norm_g", "gla_norm_g", "ret_norm_g")], axis=1)
    put("normg", ng.transpose(3, 0, 1, 2))
    lb = np.asarray(inp["hgrn_lb_logits"], np.float32).reshape(NL, 2, 128)
    put("lbl", lb.transpose(2, 0, 1))
    put("bgc", np.asarray(inp["gla_b_gate"], np.float32).T)
    pp4 = np.concatenate([np.asarray(inp["delta_a_log"], np.float32).T,
                          np.asarray(inp["delta_dt_bias"], np.float32).T], axis=1)
    pp16 = np.asarray(inp["gla_w_gate"], np.float32).transpose(1, 0, 2).reshape(16, NL * 128)
    return a, np.ascontiguousarray(pp4), np.ascontiguousarray(pp16)


def build_program(nlayers=NL, dbg=None):
    nc = bass.Bass("TRN2", target_bir_lowering=False)
    c_off, c_tot = _c128_offsets()
    p_off, p_tot = _pp_offsets()

    def din(name, shape):
        return nc.dram_tensor(name, list(shape), F32, kind="ExternalInput").ap()

    def dout(name, shape):
        return nc.dram_tensor(name, list(shape), F32, kind="ExternalOutput").ap()

    xp_d = din("xp", [SEQ, D])
    xs_d = din("xs", [NSAMP, D])
    meta_d = din("meta", [NMETA, D])
    sconv_d = din("sconv", [NL, 48, 768])
    sdelta_d = din("sdelta", [NL, 16, 4, 64, 64])
    shgrn_d = din("shgrn", [NL, 16, 4, 64, 64])
    sgla_d = din("sgla", [NL, 16, 4, 32, 64])
    sret_d = din("sret", [NL, 16, 4, 64, 64])
    win_d = din("w_in", [NL, D, DIN])
    wout_d = din("w_out", [NL, D, D])
    wg_d = din("w_g", [NL, D, DFF])
    wu_d = din("w_u", [NL, D, DFF])
    wd_d = din("w_d", [NL, DFF, D])
    pp128_d = din("pp128", [128, p_tot])
    pp4_d = din("pp4", [4, 4])
    pp16_d = din("pp16", [16, NL * 128])
    c128_d = din("c128", [128, c_tot])
    c4_d = din("c4", [4, 389])
    c8_d = din("c8", [8, 4])
    cos_d = din("cosT", [128, NTOK])
    sin_d = din("sinT", [128, NTOK])

    yp_d = dout("y_p", [SEQ, D])
    ys_d = dout("y_s", [NSAMP, D])
    convp_d = dout("conv_p", [NL, 3, 768])
    convs_d = dout("conv_s", [NL, 16, 3, 768])
    sp_out = {"A": dout("delta_p", [NL, 4, 64, 64]), "B": dout("hgrn_p", [NL, 4, 64, 64]),
              "C": dout("gla_p", [NL, 4, 32, 64]), "D": dout("ret_p", [NL, 4, 64, 64])}
    ss_out = {"A": dout("delta_s", [NL, 16, 4, 64, 64]), "B": dout("hgrn_s", [NL, 16, 4, 64, 64]),
              "C": dout("gla_s", [NL, 16, 4, 32, 64]), "D": dout("ret_s", [NL, 16, 4, 64, 64])}
    ss_in = {"A": sdelta_d, "B": shgrn_d, "C": sgla_d, "D": sret_d}
    dbg_d = dout("dbg", [128, 4096]) if dbg else None

    SLAB_LIST = []
    for (c0_, c1_, _chs) in SLABS:
        SLAB_LIST.append(("in", c0_, c1_, 0, 8))
    for h_ in range(4):
        SLAB_LIST.append(("out", h_ * 256, (h_ + 1) * 256, 0, 8))
    for s_ in range(11):
        SLAB_LIST.append(("g", s_ * 256, (s_ + 1) * 256, 0, 8))
        SLAB_LIST.append(("u", s_ * 256, (s_ + 1) * 256, 0, 8))
    for m_ in range(8):
        SLAB_LIST.append(("d", m_ * 128, (m_ + 1) * 128, 0, 11))
        SLAB_LIST.append(("d", m_ * 128, (m_ + 1) * 128, 11, 11))
    SLAB_ID = {(n_, a_, k0_): i_ for i_, (n_, a_, b_, k0_, k_) in enumerate(SLAB_LIST)}
    wslab = nc.dram_tensor("wslab", [NL, len(SLAB_LIST), 128, 2048], BF16).ap()
    wsrc = {"in": win_d, "out": wout_d, "g": wg_d, "u": wu_d, "d": wd_d}

    st = ExitStack()
    with st:
        T = Tracker(nc, st)
        for l_ in range(nlayers):
            for i_, (n_, a_, b_, k0_, k_) in enumerate(SLAB_LIST):
                w_ = b_ - a_
                T.dma((wslab[l_, i_, :, 0:k_ * w_].rearrange("p (k c) -> p k c", k=k_), (l_, i_)),
                      wsrc[n_][l_].rearrange("(kc p) c -> p kc c", p=128)[:, k0_:k0_ + k_, a_:b_], q="pool")
        _n = [0]

        def sb(shape, name=None, dt=F32):
            _n[0] += 1
            return st.enter_context(nc.sbuf_tensor("sb_" + (name or f"t{_n[0]}"), list(shape), dt))

        banks = [st.enter_context(nc.psum_tensor(f"ps{i}", [128, 512], F32)) for i in range(8)]
        rr = {"big": [0, (0, 1, 2)], "prep": [0, (3, 4)], "G": [0, (5, 6)], "st6": [0, (0, 1, 2, 5, 6)]}

        def bank(kind):
            i, ids = rr[kind]
            rr[kind][0] = (i + 1) % len(ids)
            return banks[ids[i]]
        B_O, B_S, B_D = banks[5], banks[6], banks[7]

        c128 = sb([128, c_tot], "c128")
        pp = sb([128, p_tot], "pp128")
        c4 = sb([4, 389], "c4")
        c8 = sb([8, 4], "c8")
        pp4 = sb([4, 4], "pp4")
        wgc = sb([16, NL * 128], "wgc")
        T.dma(c128[:], c128_d)
        T.dma(pp[:], pp128_d)
        T.dma(c4[:], c4_d)
        T.dma(c8[:], c8_d)
        T.dma(pp4[:], pp4_d)
        T.dma(wgc[:], pp16_d)

        def C(name, lo=0, hi=None, rows=slice(0, 128)):
            o, w = c_off[name]
            hi = w if hi is None else hi
            return c128[rows, o + lo:o + hi]

        def P(name, lo=0, hi=None):
            o, w = p_off[name]
            hi = w if hi is None else hi
            return pp[:, o + lo:o + hi]
        ident = C("ident")
        I2 = C("I2")
        hm = c4[:, 0:4]
        onesrow = c4[:, 260:388]
        ones4 = c4[:, 388:389]

        def hsel(pr):
            return c4[:, 4 + pr * 128: 4 + (pr + 1) * 128]

        nA = sb([4, 2], "nA")
        T.act(nA[:], pp4[:, 0:2], AF.Exp)
        T.ts(nA[:], nA[:], -1.0, ALU.mult)
        rlb = sb([128, 4], "rlb")
        T.memset(rlb[:], 1.0)
        dl = sb([128, 2], "dl")
        T.tt(dl[:], P("lbl", 2, 4), P("lbl", 0, 2), ALU.subtract)
        T.act(dl[:], dl[:], AF.Exp)
        T.ts(rlb[:, 2:4], dl[:], 1.0, ALU.add)
        negbg = sb([128, 2], "negbg")
        T.ts(negbg[:], P("bgc"), -1.0, ALU.mult)

        FW = NT + 8
        xT = sb([128, 8, NT], "xT")
        x1T = sb([128, 8, NT], "x1T")
        oT = sb([128, 8, NT], "oT", BF16)
        xTb = sb([128, 8, NT], "xTb", BF16)
        x1Tb = sb([128, 8, NT], "x1Tb", BF16)
        slabs = [sb([128, 2048], f"slab{i}", BF16) for i in range(4)]
        slab_i = [0]

        def next_slab():
            s = slabs[slab_i[0]]
            slab_i[0] = (slab_i[0] + 1) % len(slabs)
            return s
        NFM = 41
        fmt = [sb([128, FW], f"fm{i}") for i in range(NFM)]
        cb = fmt[0:6]
        cs = fmt[6:12]
        sg = {"A": fmt[12:14], "B": fmt[14:16], "C": fmt[16:18], "D": fmt[18:20]}
        qk = {("B", "q"): fmt[20:22], ("B", "k"): fmt[22:24], ("B", "g"): fmt[24:26],
              ("C", "q"): fmt[26:27], ("C", "k"): fmt[27:28], ("C", "g"): fmt[28:29],
              ("D", "q"): fmt[29:31], ("D", "k"): fmt[31:33]}
        tmp_pool = fmt[33:NFM]
        tmp_i = [0]

        def tmp():
            t_ = tmp_pool[tmp_i[0]]
            tmp_i[0] = (tmp_i[0] + 1) % len(tmp_pool)
            return t_
        tmpg_i, tmpd_i = [0], [0]

        def tmpG():
            t_ = tmp_pool[tmpg_i[0]]
            tmpg_i[0] = (tmpg_i[0] + 1) % 6
            return t_

        def tmpD():
            t_ = tmp_pool[6 + tmpd_i[0]]
            tmpd_i[0] = (tmpd_i[0] + 1) % 2
            return t_
        hT = [t_[:].bitcast(BF16) for t_ in fmt[6:28]]
        cbs = [sb([128, 16, 7], f"cbs{j}") for j in range(6)]
        hist = [sb([128, 6, 3], f"hist{l}") for l in range(nlayers)]
        ba8 = sb([8, FW], "ba8")
        lrT = sb([16, FW], "lrT")
        s4g = sb([4, FW], "s4g")
        s4cs = sb([4, FW], "s4cs")
        s4q = sb([4, 5, FW], "s4q")
        ebl4 = sb([4, 24], "ebl4")
        rowx = [sb([4, 5, 128], f"rowx{i}") for i in range(2)]
        rowx_i = [0]

        def next_rowx():
            t_ = rowx[rowx_i[0]]
            rowx_i[0] = (rowx_i[0] + 1) % 2
            return t_
        MIX = ("A", "B", "C", "D")
        Sp = {(l, m, pr): sb([128, 64], f"Sp{l}{m}{pr}") for l in range(nlayers) for m in MIX for pr in range(2)}
        for k_, v_ in Sp.items():
            T.memset(v_[:], 0.0)
        Sspool = [sb([128, 16, 64], f"Ss{i}") for i in range(2)]
        ss_i = [0]
        cstage = Sspool[0][:].rearrange("p a b -> p (a b)")

        def next_ss():
            t_ = Sspool[ss_i[0]]
            ss_i[0] = (ss_i[0] + 1) % len(Sspool)
            return t_
        vpair = {(m, pr): sb([128, 4, 64], f"vp{m}{pr}") for m in MIX for pr in range(2)}
        bdq = {pr: sb([128, 4, 128], f"bdq{pr}") for pr in range(2)}
        attT = {pr: sb([128, 4, 128], f"att{pr}") for pr in range(2)}
        kpair = {pr: sb([128, 4, 128], f"kp{pr}") for pr in range(2)}
        bdqG = {pr: sb([128, 4, 128], f"bdqG{pr}") for pr in range(2)}
        attTG = {pr: sb([128, 4, 128], f"attG{pr}") for pr in range(2)}
        kpairG = {pr: sb([128, 4, 128], f"kpG{pr}") for pr in range(2)}
        redG = sb([128, 128], "redG")
        bdtmp = [sb([128, 4, 128], f"bdt{i}") for i in range(11)]
        bdt_i = [0]

        def next_bdt():
            t_ = bdtmp[bdt_i[0]]
            bdt_i[0] = (bdt_i[0] + 1) % 9
            return t_
        bdtg_i = [0]

        def next_bdtG():
            t_ = bdtmp[9 + bdtg_i[0]]
            bdtg_i[0] = (bdtg_i[0] + 1) % 2
            return t_
        eblc = {pr: sb([128, 24], f"eblc{pr}") for pr in range(2)}
        o_allA = sb([128, 4, 2, 64], "o_allA")
        o_allG = sb([128, 4, 6, 64], "o_allG")
        xin = [o_allG[:].rearrange("p a b c -> p (a b c)")[:, 0:D]]
        vexp = [sb([128, 16, 64], f"vexp{i}") for i in range(1)]
        in_stage = [Sspool[1][:].rearrange("p a b -> p (a b)"), vexp[0][:].rearrange("p a b -> p (a b)")]
        out_stage = [Sspool[0][:].rearrange("p a b -> p (a b)"), xin[0]]
        prefetched = {}
        kTbd = {pr: sb([128, 4, 128], f"kTbd{pr}") for pr in range(2)}
        TT = {pr: sb([128, 4, 128], f"TT{pr}") for pr in range(2)}
        colsb = sb([128, 4, 2, 8], "colsb")
        eblS = {pr: sb([128, 16], f"eblS{pr}") for pr in range(2)}
        Sch = {pr: sb([128, 4, 64], f"Sch{pr}") for pr in range(2)}
        small = [sb([128, 128], f"sm{i}") for i in range(6)]
        sm_i = [0]

        def next_sm():
            t_ = small[sm_i[0]]
            sm_i[0] = (sm_i[0] + 1) % len(small)
            return t_
        cosb = sb([128, NT], "cosb")
        sinb = sb([128, NT], "sinb")
        nstatA = sb([128, 4, 2, 2], "nstatA")
        nstatG = sb([128, 4, 6, 2], "nstatG")

        def layer_norm_fm(src, dst, dstb, gname, bname, goff, nt):
            b1 = bank("big")
            b2 = bank("big")
            for kc in range(8):
                T.matmul(b1[:, 0:nt], C("onesM"), src[:, kc, 0:nt], start=(kc == 0), stop=(kc == 7))
            s2 = bdtmp[4][:].rearrange("p a b -> p (a b)")
            for kc in range(4):
                sq = tmp()
                T.tt(sq[:, 0:nt], src[:, kc, 0:nt], src[:, kc, 0:nt], ALU.mult, eng="dve")
                T.matmul(b2[:, 0:nt], C("onesM"), sq[:, 0:nt], start=(kc == 0), stop=False)
            T.tt(s2[:, 0:nt], src[:, 4, 0:nt], src[:, 4, 0:nt], ALU.mult, eng="dve")
            for kc in range(5, 8):
                sq = tmp()
                T.tt(sq[:, 0:nt], src[:, kc, 0:nt], src[:, kc, 0:nt], ALU.mult, eng="dve")
                T.tt(s2[:, 0:nt], s2[:, 0:nt], sq[:, 0:nt], ALU.add, eng="dve")
            T.matmul(b2[:, 0:nt], C("onesM"), s2[:, 0:nt], start=False, stop=True)
            mean, m2, var, rstd = (bdtmp[i_][:].rearrange("p a b -> p (a b)") for i_ in range(4))
            T.copy(mean[:, 0:nt], b1[:, 0:nt], eng="act")
            T.tt(m2[:, 0:nt], mean[:, 0:nt], mean[:, 0:nt], ALU.mult, eng="dve")
            T.tt(var[:, 0:nt], b2[:, 0:nt], m2[:, 0:nt], ALU.subtract)
            T.ts(var[:, 0:nt], var[:, 0:nt], LN_EPS, ALU.add)
            T.act(var[:, 0:nt], var[:, 0:nt], AF.Ln)
            T.act(rstd[:, 0:nt], var[:, 0:nt], AF.Exp, scale=-0.5)
            for kc in range(8):
                t1 = tmp()
                T.tt(t1[:, 0:nt], src[:, kc, 0:nt], mean[:, 0:nt], ALU.subtract)
                T.tt(t1[:, 0:nt], t1[:, 0:nt], rstd[:, 0:nt], ALU.mult, eng="dve")
                T.act(dst[:, kc, 0:nt], t1[:, 0:nt], AF.Identity, bias=P(bname, goff + kc, goff + kc + 1),
                      scale=P(gname, goff + kc, goff + kc + 1))
                T.act(dstb[:, kc, 0:nt], t1[:, 0:nt], AF.Identity, bias=P(bname, goff + kc, goff + kc + 1),
                      scale=P(gname, goff + kc, goff + kc + 1))

        def load_slab(name, l_, c0_, kc, w, k0=0):
            s_ = next_slab()
            i_ = SLAB_ID[(name, c0_, k0)]
            T.dma(s_[:, 0:kc * w], (wslab[l_, i_, :, 0:kc * w], (l_, i_)), q="sp")
            return s_[:, 0:kc * w].rearrange("p (k c) -> p k c", k=kc)

        for ti in range(NTILES):
            col0 = ti * NT
            nt = NT if ti < 8 else 128
            nreg = nt if ti < 8 else 64
            has_s = ti == 8
            nchr = nreg // 64
            segs = [(0, nchr, 64)] + ([(64, 16, 4)] if has_s else [])
            chunks = [("r", c) for c in range(nchr)] + ([("s", nchr)] if has_s else [])

            def load_block(ti_, blk_):
                xi_ = in_stage[blk_]
                c_lo_ = ti_ * NT + blk_ * 128
                if c_lo_ >= NPROMPT:
                    T.dma(xi_[0:64, :], xs_d)
                    return xi_, 64
                if ti_ == 0 and blk_ == 0:
                    T.memset(xi_[0:NPAD, :], 0.0)
                    T.dma(xi_[NPAD:NPAD + NMETA, :], meta_d)
                    T.dma(xi_[64:128, :], xp_d[0:64, :])
                elif ti_ == 8:
                    r0_ = c_lo_ - 64
                    T.dma(xi_[0:64, :], xp_d[r0_:r0_ + 64, :])
                    T.dma(xi_[64:128, :], xs_d)
                else:
                    r0_ = c_lo_ - 64
                    T.dma(xi_[:, :], xp_d[r0_:r0_ + 128, :])
                return xi_, 128

            for blk in range(nt // 128):
                if (ti, blk) in prefetched:
                    xi, nrow = prefetched.pop((ti, blk))
                else:
                    xi, nrow = load_block(ti, blk)
                for kc in range(8):
                    b = bank("big")
                    T.transpose(b[:, 0:nrow], xi[0:nrow, kc * 128:(kc + 1) * 128], ident[0:nrow, 0:nrow])
                    T.copy(x1T[:, kc, blk * 128: blk * 128 + nrow], b[:, 0:nrow], eng=("act" if kc % 2 else "dve"))
            layer_norm_fm(x1T, xT, xTb, "embg", "embb", 0, nt)
            T.dma(cosb[:, 0:nt], cos_d[:, col0:col0 + nt])
            T.dma(sinb[:, 0:nt], sin_d[:, col0:col0 + nt])

            for l in range(nlayers):
                if ti == 0:
                    T.memset(xT[:, :, 0:NPAD], 0.0)
                    T.memset(xTb[:, :, 0:NPAD], 0.0)
                    for j in range(6):
                        T.memset(cb[j][:, 0:3], 0.0)
                else:
                    for j in range(6):
                        T.copy(cb[j][:, 0:3], hist[l][:, j, :], eng="dve")
                if has_s:
                    xi = xin[0]
                    T.dma(xi[0:48, 0:768], sconv_d[l])
                    for j in range(6):
                        b = bank("big")
                        T.transpose(b[:, 0:48], xi[0:48, j * 128:(j + 1) * 128], ident[0:48, 0:48])
                        T.copy(cbs[j][:, :, 0:3], b[:, 0:48].rearrange("p (s j) -> p s j", j=3), eng="dve")

                nch = nt // 64

                def segv(t_, off, s0, ng, gl, rows=slice(0, 128)):
                    return t_[rows, off + s0: off + s0 + ng * gl].rearrange("p (g t) -> p g t", t=gl)

                def local_cumsum(gF, csb, bT, rows=slice(0, 128), np_=128):
                    T.memset(csb[rows, 0:1], 0.0)
                    T.scan(csb[rows, 1:1 + nt], gF[rows, 0:nt])
                    for (s0, ng, gl) in segs:
                        T.tt(segv(bT, 0, s0, ng, gl, rows), segv(csb, 1, s0, ng, gl, rows),
                             segv(csb, 0, s0, ng, gl, rows)[:, :, 0:1].to_broadcast([np_, ng, gl]),
                             ALU.subtract, eng="dve")

                def bl_minus_b(d3, bT, rows=slice(0, 128), np_=128):
                    for (s0, ng, gl) in segs:
                        T.tt(segv(d3, 0, s0, ng, gl, rows),
                             segv(bT, 0, s0, ng, gl, rows)[:, :, gl - 1:gl].to_broadcast([np_, ng, gl]),
                             segv(bT, 0, s0, ng, gl, rows), ALU.subtract, eng="dve")

                def exp_bl(dst, bT, rows=slice(0, 128)):
                    for (s0, ng, gl) in segs:
                        off = 0 if gl == 64 else 4
                        T.act(dst[rows, off:off + ng], segv(bT, 0, s0, ng, gl, rows)[:, :, gl - 1], AF.Exp)

                def sample_inter(lhsT_bd, Sb, red, pool="big"):
                    vx = vexp[0]
                    for h_ in range(2):
                        b = bank(pool)
                        T.matmul(b[:, 0:512], lhsT_bd, Sb[:, 8 * h_:8 * h_ + 8, :].rearrange("p s v -> p (s v)"))
                        T.tt(vx[:, 8 * h_:8 * h_ + 8, :], b[:, 0:512].rearrange("p (s v) -> p s v", v=64),
                             C("rm16", 8 * h_, 8 * h_ + 8).to_broadcast([128, 8, 64]) if False else
                             C("rm16", 8 * h_, 8 * h_ + 8).unsqueeze(2).to_broadcast([128, 8, 64]), ALU.mult)
                    T.op("dve", lambda e: e.tensor_reduce(red, vx[:].rearrange("p s v -> p v s"), AX.X, ALU.add),
                         [vx[:]], [red])

                def sample_state_update(Sb, eb, pairs, pool="big"):
                    T.tt(Sb[:], Sb[:], eb.unsqueeze(2).to_broadcast([128, 16, 64]), ALU.mult, eng="dve")
                    for h_ in range(2):
                        b = bank(pool)
                        for i_, (kp_, x_) in enumerate(pairs):
                            vx = vexp[0]
                            T.tt(vx[:, 0:8, :], x_.unsqueeze(1).to_broadcast([128, 8, 64]),
                                 C("rm16", 8 * h_, 8 * h_ + 8).unsqueeze(2).to_broadcast([128, 8, 64]), ALU.mult, eng="dve")
                            T.matmul(b[:, 0:512], kp_, vx[:, 0:8, :].rearrange("p s v -> p (s v)"),
                                     start=(i_ == 0), stop=(i_ == len(pairs) - 1))
                        T.tt(Sb[:, 8 * h_:8 * h_ + 8, :], Sb[:, 8 * h_:8 * h_ + 8, :],
                             b[:, 0:512].rearrange("p (s v) -> p s v", v=64), ALU.add)

                def state_in(m, pr):
                    if m == "C":
                        return ss_in[m][l].rearrange("s h k v -> (h k) s v")
                    return ss_in[m][l, :, 2 * pr:2 * pr + 2].rearrange("s h k v -> (h k) s v")

                def state_out(m, pr):
                    if m == "C":
                        return ss_out[m][l].rearrange("s h k v -> (h k) s v")
                    return ss_out[m][l, :, 2 * pr:2 * pr + 2].rearrange("s h k v -> (h k) s v")

                def head_norm(oa_t, nst, slot0, nsl, dsl, scratch, bdt_fn, bname):
                    oa = oa_t[:, 0:nch, :, :]
                    if dsl:
                        d0, d1 = dsl[0], dsl[-1] + 1
                        T.op("dve", lambda e: e.tensor_reduce(nst[:, 0:nch, d0:d1, 0], oa_t[:, 0:nch, d0:d1, :], AX.X, ALU.add),
                             [oa_t[:]], [nst[:]])
                        T.ts(nst[:, 0:nch, d0:d1, 0], nst[:, 0:nch, d0:d1, 0], 1.0 / 64.0, ALU.mult)
                        T.tt(oa_t[:, 0:nch, d0:d1, :], oa_t[:, 0:nch, d0:d1, :],
                             nst[:, 0:nch, d0:d1, 0:1].to_broadcast([128, nch, d1 - d0, 64]), ALU.subtract)
                    osq = scratch[:].rearrange("p a b -> p (a b)")[:, 0:nch * 128].rearrange("p (c v) -> p c v", v=64)
                    for half in range(nsl // 2):
                        T.tt(osq[:, 0:nch * 2, :].rearrange("p (c s) v -> p c s v", s=2), oa_t[:, 0:nch, 2 * half:2 * half + 2, :],
                             oa_t[:, 0:nch, 2 * half:2 * half + 2, :], ALU.mult)
                        T.op("dve", lambda e, half=half: e.tensor_reduce(
                            nst[:, 0:nch, 2 * half:2 * half + 2, 1],
                            osq[:, 0:nch * 2, :].rearrange("p (c s) v -> p c s v", s=2), AX.X, ALU.add), [scratch[:]], [nst[:]])
                    yield
                    nr = nsl - len(dsl)
                    T.ts(nst[:, 0:nch, 0:nr, 1], nst[:, 0:nch, 0:nr, 1], 1.0 / 64.0, ALU.mult, NORM_EPS, ALU.add)
                    if dsl:
                        T.ts(nst[:, 0:nch, nr:nsl, 1], nst[:, 0:nch, nr:nsl, 1], 1.0 / 64.0, ALU.mult, LN_EPS, ALU.add)
                    T.act(nst[:, 0:nch, :, 1], nst[:, 0:nch, :, 1], AF.Ln)
                    T.act(nst[:, 0:nch, :, 1], nst[:, 0:nch, :, 1], AF.Exp, scale=-0.5)
                    T.tt(oa, oa, nst[:, 0:nch, :, 1:2].to_broadcast([128, nch, nsl, 64]), ALU.mult)
                    yield
                    for sl_ in range(nsl):
                        slot = slot0 + sl_
                        m, pr = MIX[slot // 2], slot % 2
                        obd = bdt_fn()
                        for blk in range(2):
                            T.ts(obd[:, 0:nch, blk * 64:(blk + 1) * 64], oa_t[:, 0:nch, sl_, :], C("rm64", blk, blk + 1), ALU.mult)
                        pb = bank(bname)
                        for c in range(nch):
                            T.matmul(pb[:, c * 64:(c + 1) * 64], obd[:, c, :], I2)
                        ng_ = (l * 4 + slot // 2) * 2 + pr
                        T.stt(oT[:, slot, 0:nt], pb[:, 0:nt], P("normg", ng_, ng_ + 1), sg[m][pr][:, 0:nt], ALU.mult, ALU.mult)
                        yield

                def sqb(j_):
                    return bdtmp[9 + j_ // 2][:].rearrange("p a b -> p (a b)")[:, (j_ % 2) * 256:(j_ % 2 + 1) * 256]

                def gen_A():
                    for j in range(6):
                        acc = tmpD()

                        def wtap(tap):
                            o_ = (l * 6 + j) * 4 + tap
                            return P("convw", o_, o_ + 1)
                        T.ts(acc[:, 0:nreg], cb[j][:, 0:nreg], wtap(0), ALU.mult, eng="dve")
                        for tap in range(1, 4):
                            T.stt(acc[:, 0:nreg], cb[j][:, tap:tap + nreg], wtap(tap), acc[:, 0:nreg],
                                  ALU.mult, ALU.add, eng="dve")
                        if has_s:
                            av = acc[:, 64:128].rearrange("p (s t) -> p s t", t=4)
                            T.ts(av, cbs[j][:, :, 0:4], wtap(0), ALU.mult, eng="dve")
                            for tap in range(1, 4):
                                T.stt(av, cbs[j][:, :, tap:tap + 4], wtap(tap), av, ALU.mult, ALU.add, eng="dve")
                        T.act(cs[j][:, 0:nt], acc[:, 0:nt], AF.Silu)
                        yield
                        if j < 4:
                            T.tt(sqb(j)[:, 0:nt], cs[j][:, 0:nt], cs[j][:, 0:nt], ALU.mult, eng="dve")
                        if has_s:
                            cst = tmpD()
                            T.copy(cst[:, 0:3], cb[j][:, nreg:nreg + 3], eng="dve")
                            T.copy(cst[:, 3:51].rearrange("p (s j) -> p s j", j=3), cbs[j][:, :, 4:7], eng="dve")
                            bt_ = banks[7]
                            T.transpose(bt_[0:51, 0:128], cst[:, 0:51], ident)
                            T.copy(cstage[0:51, j * 128:(j + 1) * 128], bt_[0:51, 0:128], eng="act")
                            if j == 5:
                                T.dma(convp_d[l], cstage[0:3, 0:768])
                                T.dma(convs_d[l].rearrange("s j f -> (s j) f"), cstage[3:51, 0:768])
                        else:
                            T.copy(hist[l][:, j, :], cb[j][:, nreg:nreg + 3], eng="dve")
                    nb = [banks[7][:, 0:nt], banks[7][:, 256:256 + nt], banks[3][:, 0:nt], banks[3][:, 256:256 + nt]]
                    for j in range(4):
                        T.matmul(nb[j], C("bones"), sqb(j)[:, 0:nt])
                        if j < 3:
                            yield
                    for j in range(4):
                        T.ts(sqb(j)[:, 0:nt], nb[j], NORM_EPS, ALU.add)
                    for j in range(4):
                        T.act(sqb(j)[:, 0:nt], sqb(j)[:, 0:nt], AF.Ln)
                    for j in range(4):
                        T.act(sqb(j)[:, 0:nt], sqb(j)[:, 0:nt], AF.Exp, scale=-0.5)
                    for j in range(4):
                        if j < 2:
                            T.stt(cs[j][:, 0:nt], cs[j][:, 0:nt], 0.125, sqb(j)[:, 0:nt], ALU.mult, ALU.mult, eng="dve")
                        else:
                            T.tt(cs[j][:, 0:nt], cs[j][:, 0:nt], sqb(j)[:, 0:nt], ALU.mult, eng="dve")
                    yield
                    r4 = slice(0, 4)
                    b = banks[7]
                    T.matmul(b[0:4, 0:nt], c8[:, 0:4], ba8[:, 0:nt])
                    T.act(s4g[:, 0:nt], b[0:4, 0:nt], AF.Exp, bias=pp4[:, 2 + l:3 + l])
                    T.ts(s4g[:, 0:nt], s4g[:, 0:nt], 1.0, ALU.add)
                    T.act(s4g[:, 0:nt], s4g[:, 0:nt], AF.Ln)
                    T.ts(s4g[:, 0:nt], s4g[:, 0:nt], nA[:, l:l + 1], ALU.mult)
                    T.act(s4q[:, 2, 0:nt], ba8[0:4, 0:nt], AF.Exp, scale=-1.0)
                    T.ts(s4q[:, 2, 0:nt], s4q[:, 2, 0:nt], 1.0, ALU.add)
                    T.recip(s4q[:, 2, 0:nt], s4q[:, 2, 0:nt])
                    bT4 = s4q[:, 0, :]
                    local_cumsum(s4g, s4cs, bT4, r4, 4)
                    T.ts(s4q[:, 1, 0:nt], s4q[:, 0, 0:nt], -1.0, ALU.mult, eng="dve")
                    T.act(s4q[:, 3, 0:nt], s4q[:, 0, 0:nt], AF.Exp)
                    bl_minus_b(s4q[:, 4, :], bT4, r4, 4)
                    T.act(s4q[:, 4, 0:nt], s4q[:, 4, 0:nt], AF.Exp)
                    exp_bl(ebl4, bT4, r4)
                    yield
                    for pr in range(2):
                        qbd, kbd, vbd = bdq[pr], kTbd[pr], next_bdt()
                        for blk in range(2):
                            rmc = C("rm64", blk, blk + 1)
                            sl = slice(blk * 64, (blk + 1) * 64)
                            v3 = lambda t_: t_[:, 0:nt].rearrange("p (c t) -> p c t", t=64)
                            T.act(qbd[:, 0:nch, sl], v3(cs[pr]), AF.Copy, scale=rmc)
                            T.ts(kbd[:, 0:nch, sl], v3(cs[2 + pr]), rmc, ALU.mult)
                            T.act(vbd[:, 0:nch, sl], v3(cs[4 + pr]), AF.Copy, scale=rmc)
                        pb = bank("prep")
                        for c in range(nch):
                            T.matmul(pb[:, c * 64:(c + 1) * 64], vbd[:, c, :], I2)
                        T.copy(vpair[("A", pr)][:, 0:nch, :], pb[:, 0:nch * 64].rearrange("p (c v) -> p c v", v=64), eng="act")
                        yield
                        P1, P2, P3, P4, P5 = banks[0], banks[1], banks[2], banks[3], banks[4]
                        for (kind, c) in chunks:
                            rx = next_rowx()
                            T.tt(rx[:].rearrange("p q (h t) -> p q h t", t=64),
                                 s4q[:, :, c * 64:(c + 1) * 64].unsqueeze(2).to_broadcast([4, 5, 2, 64]),
                                 hm[:, 2 * pr:2 * pr + 2].unsqueeze(1).unsqueeze(3).to_broadcast([4, 5, 2, 64]),
                                 ALU.mult)
                            Bx, NBx, Betax, Gamx, Wx = (rx[:, i_, :] for i_ in range(5))
                            cc = slice(c * 128, (c + 1) * 128)
                            T.matmul(P5[:, c * 8 + 0:c * 8 + 1], Betax, ones4)
                            T.matmul(P5[:, c * 8 + 1:c * 8 + 2], Gamx, ones4)
                            T.matmul(P5[:, c * 8 + 2:c * 8 + 3], Wx, ones4)
                            if kind == "r":
                                T.matmul(P5[:, c * 8 + 3:c * 8 + 4], hsel(pr), ebl4[:, c:c + 1])
                            else:
                                T.matmul(P5[:, c * 8 + 3:c * 8 + 4], hsel(pr), ebl4[:, 4:5])
                                T.matmul(P5[:, 64:80], hsel(pr), ebl4[:, 4:20])
                            T.matmul(P1[:, cc], hsel(pr), Bx, start=True, stop=False)
                            T.matmul(P1[:, cc], NBx, hsel(pr), start=False, stop=True)
                            T.matmul(P2[:, cc], kbd[:, c, :], kbd[:, c, :])
                            T.matmul(P3[:, cc], kbd[:, c, :], qbd[:, c, :])
                            T.matmul(P4[:, cc], onesrow, Betax)
                            yield
                        T.copy(colsb[:, 0:nch, pr, 0:4], P5[:, 0:nch * 8].rearrange("p (c j) -> p c j", j=8)[:, :, 0:4], eng="act")
                        if has_s:
                            T.copy(eblS[pr][:], P5[:, 64:80], eng="act")
                        T.ts(colsb[:, 0:nch, pr, 4:6], colsb[:, 0:nch, pr, 0:2], -1.0, ALU.mult)
                        v4 = lambda bk: bk[:, 0:nch * 128].rearrange("p (c t) -> p c t", t=128)
                        dT, t1, t2 = next_bdt(), next_bdt(), next_bdt()
                        T.ts(dT[:, 0:nch, :], v4(P1), 0.0, ALU.min)
                        T.act(dT[:, 0:nch, :], dT[:, 0:nch, :], AF.Exp)
                        yield
                        T.tt(t1[:, 0:nchr, :], dT[:, 0:nchr, :], C("mTi").unsqueeze(1).to_broadcast([128, nchr, 128]), ALU.mult)
                        T.tt(t2[:, 0:nchr, :], dT[:, 0:nchr, :], C("mTs").unsqueeze(1).to_broadcast([128, nchr, 128]), ALU.mult)
                        if has_s:
                            T.tt(t1[:, nchr, :], dT[:, nchr, :], C("smTi"), ALU.mult)
                            T.tt(t2[:, nchr, :], dT[:, nchr, :], C("smTs"), ALU.mult)
                        T.tt(attT[pr][:, 0:nch, :], v4(P3), t1[:, 0:nch, :], ALU.mult)
                        T.tt(t2[:, 0:nch, :], v4(P2), t2[:, 0:nch, :], ALU.mult)
                        yield
                        MT = next_bdt()
                        T.stt(MT[:, 0:nch, :], v4(P4), -1.0, t2[:, 0:nch, :], ALU.mult, ALU.mult)
                        for c in range(nch):
                            T.transpose(P1[:, c * 128:(c + 1) * 128], MT[:, c, :], ident)
                        Mm = next_bdt()
                        T.copy(Mm[:, 0:nch, :], v4(P1), eng="act")
                        X = next_bdt()
                        T.tt(X[:, 0:nch, :], MT[:, 0:nch, :], ident.unsqueeze(1).to_broadcast([128, nch, 128]), ALU.add)
                        for j in range(1, 6):
                            Q1, Q2, Q3 = (banks[2], banks[3], banks[4]) if j % 2 else (banks[0], banks[1], banks[4])
                            for c in range(nch):
                                T.matmul(Q1[:, c * 128:(c + 1) * 128], MT[:, c, :], Mm[:, c, :])
                            if j < 5:
                                for c in range(nch):
                                    T.matmul(Q2[:, c * 128:(c + 1) * 128], Mm[:, c, :], MT[:, c, :])
                            Mn = next_bdt()
                            T.copy(Mn[:, 0:nch, :], v4(Q1), eng="act")
                            yield
                            if j < 5:
                                MTn = next_bdt()
                                T.copy(MTn[:, 0:nch, :], v4(Q2), eng="act")
                            for c in range(nch):
                                T.matmul(Q3[:, c * 128:(c + 1) * 128], Mn[:, c, :], X[:, c, :])
                            Xn = next_bdt() if j < 5 else TT[pr]
                            T.tt(Xn[:, 0:nch, :], v4(Q3), X[:, 0:nch, :], ALU.add)
                            yield
                            if j < 5:
                                Mm, MT, X = Mn, MTn, Xn
                        for c in range(nch):
                            T.transpose(P1[:, c * 128:(c + 1) * 128], kbd[:, c, :], ident)
                        T.tt(kpair[pr][:, 0:nch, :], v4(P1), colsb[:, 0:nch, pr, 2:3].to_broadcast([128, nch, 128]), ALU.mult)
                        yield
                    for (kind, c) in chunks:
                        for pr in range(2):
                            slot = pr
                            BD = banks[7] if pr == 0 else banks[4]
                            gamc, ngamc, nbetac, eblc_ = (colsb[:, c, pr, 1:2], colsb[:, c, pr, 5:6],
                                                          colsb[:, c, pr, 4:5], colsb[:, c, pr, 3:4])
                            tr = next_sm()
                            if kind == "r":
                                S = Sp[(l, "A", pr)]
                                T.matmul(BD[:, 0:64], kTbd[pr][:, c, :], S[:])
                                T.stt(tr[:, 0:64], BD[:, 0:64], gamc, vpair[("A", pr)][:, c, :], ALU.mult, ALU.subtract)
                            else:
                                Sb = Sspool[0]
                                T.dma(Sb[:], state_in("A", pr))
                                red = next_sm()
                                sample_inter(kTbd[pr][:, c, :], Sb, red[:, 0:64])
                                T.stt(tr[:, 0:64], red[:, 0:64], gamc, vpair[("A", pr)][:, c, :], ALU.mult, ALU.subtract)
                            T.ts(tr[:, 0:64], tr[:, 0:64], nbetac, ALU.mult)
                            yield
                            T.matmul(BD[:, 64:128], TT[pr][:, c, :], tr[:, 0:64])
                            u = next_sm()
                            T.copy(u[:, 0:64], BD[:, 64:128], eng="act")
                            yield
                            T.matmul(BD[:, 128:192], attT[pr][:, c, :], u[:, 0:64])
                            o1 = next_sm()
                            T.copy(o1[:, 0:64], BD[:, 128:192], eng="act")
                            if kind == "r":
                                T.matmul(BD[:, 192:256], bdq[pr][:, c, :], S[:])
                                T.matmul(BD[:, 256:320], kpair[pr][:, c, :], u[:, 0:64])
                                T.stt(o_allA[:, c, slot, :], BD[:, 192:256], gamc, o1[:, 0:64], ALU.mult, ALU.add)
                                T.stt(S[:], S[:], eblc_, BD[:, 256:320], ALU.mult, ALU.add)
                                yield
                            else:
                                red2 = next_sm()
                                sample_inter(bdq[pr][:, c, :], Sb, red2[:, 0:64])
                                T.stt(o_allA[:, c, slot, :], red2[:, 0:64], gamc, o1[:, 0:64], ALU.mult, ALU.add)
                                sample_state_update(Sb, eblS[pr][:], [(kpair[pr][:, c, :], u[:, 0:64])])
                                T.dma(state_out("A", pr), Sb[:])
                    if has_s:
                        for pr in range(2):
                            T.dma(sp_out["A"][l, 2 * pr:2 * pr + 2].rearrange("h k v -> (h k) v"), Sp[(l, "A", pr)][:])

                    yield
                def gen_G():
                    for m in ("B", "C", "D"):
                        mi = MIX.index(m)
                        for pr in range(2):
                            if m == "C":
                                qF, kF, gF = qk[("C", "q")][0], qk[("C", "k")][0], qk[("C", "g")][0]
                                rmn, rmi = "rm32", (2 * pr, 2 * pr + 1)
                            else:
                                qF, kF = qk[(m, "q")][pr], qk[(m, "k")][pr]
                                rmn, rmi = "rm64", (0, 1)
                                if m == "B":
                                    gF = qk[("B", "g")][pr]
                                else:
                                    gF = tmpG()
                                    T.memset(gF[:, 0:nt], 1.0)
                                    T.ts(gF[:, 0:nt], gF[:, 0:nt], C("lg", pr, pr + 1), ALU.mult, eng="dve")
                            if not (m == "C" and pr == 1):
                                csb, bT = tmpG(), tmpG()
                                local_cumsum(gF, csb, bT)
                                e1, e2, d3 = tmpG(), tmpG(), tmpG()
                                T.act(e1[:, 0:nt], bT[:, 0:nt], AF.Exp)
                                T.act(e2[:, 0:nt], bT[:, 0:nt], AF.Exp, scale=-1.0)
                                bl_minus_b(d3, bT)
                                T.act(d3[:, 0:nt], d3[:, 0:nt], AF.Exp)
                            exp_bl(eblc[pr], bT)
                            yield
                            qbd, kbd, khbd = bdqG[pr], next_bdtG(), next_bdtG()
                            v3 = lambda t_: t_[:, 0:nt].rearrange("p (c t) -> p c t", t=64)
                            for blk in range(2):
                                rmc = C(rmn, rmi[blk], rmi[blk] + 1)
                                sl = slice(blk * 64, (blk + 1) * 64)
                                T.stt(qbd[:, 0:nch, sl], v3(qF), rmc, v3(e1), ALU.mult, ALU.mult, eng="dve")
                                T.stt(kbd[:, 0:nch, sl], v3(kF), rmc, v3(e2), ALU.mult, ALU.mult, eng="dve")
                                T.stt(khbd[:, 0:nch, sl], v3(kF), rmc, v3(d3), ALU.mult, ALU.mult, eng="dve")
                                yield
                            pb = bank("G")
                            for c in range(nch):
                                T.matmul(pb[:, c * 128:(c + 1) * 128], kbd[:, c, :], qbd[:, c, :])
                            T.tt(attTG[pr][:, 0:nchr, :], pb[:, 0:nchr * 128].rearrange("p (c t) -> p c t", t=128),
                                 C("mTi").unsqueeze(1).to_broadcast([128, nchr, 128]), ALU.mult)
                            if has_s:
                                T.tt(attTG[pr][:, nchr, :], pb[:, nchr * 128:(nchr + 1) * 128], C("smTi"), ALU.mult)
                            pb = bank("G")
                            for c in range(nch):
                                T.transpose(pb[:, c * 128:(c + 1) * 128], khbd[:, c, :], ident)
                            T.copy(kpairG[pr][:, 0:nch, :], pb[:, 0:nch * 128].rearrange("p (c k) -> p c k", k=128), eng="act")
                            yield
                        for pr in range(2):
                            slot = mi * 2 + pr
                            S = Sp[(l, m, pr)]
                            psS = banks[5][:, pr * 256:(pr + 1) * 256]
                            for c in range(nchr):
                                if m == "C":
                                    T.matmul(psS[:, c * 64:(c + 1) * 64], kpairG[0][:, c, :], vpair[(m, 0)][:, c, :], start=True, stop=False)
                                    T.matmul(psS[:, c * 64:(c + 1) * 64], kpairG[1][:, c, :], vpair[(m, 1)][:, c, :], start=False, stop=True)
                                else:
                                    T.matmul(psS[:, c * 64:(c + 1) * 64], kpairG[pr][:, c, :], vpair[(m, pr)][:, c, :])
                            prev = S[:]
                            for c in range(nchr):
                                dst = Sch[pr][:, c, :]
                                T.stt(dst, prev, eblc[pr][:, c:c + 1], psS[:, c * 64:(c + 1) * 64], ALU.mult, ALU.add)
                                prev = dst
                                yield
                            po = banks[6][:, pr * 256:(pr + 1) * 256]
                            for c in range(nchr):
                                sprev = S[:] if c == 0 else Sch[pr][:, c - 1, :]
                                T.matmul(po[:, c * 64:(c + 1) * 64], attTG[pr][:, c, :], vpair[(m, pr)][:, c, :], start=True, stop=False)
                                T.matmul(po[:, c * 64:(c + 1) * 64], bdqG[pr][:, c, :], sprev, start=False, stop=True)
                            T.copy(o_allG[:, 0:nchr, slot - 2, :], po[:, 0:nchr * 64].rearrange("p (c v) -> p c v", v=64), eng="act")
                            T.copy(S[:], Sch[pr][:, nchr - 1, :], eng="dve")
                            yield
                        for (kind, c) in chunks:
                            if kind == "r":
                                continue
                            else:
                                for pr in range(2):
                                    slot = mi * 2 + pr
                                    if not (m == "C" and pr == 1):
                                        Sb = Sspool[1]
                                        T.dma(Sb[:], state_in(m, pr))
                                    red = redG
                                    sample_inter(bdqG[pr][:, c, :], Sb, red[:, 0:64], pool="G")
                                    po = banks[6][:, pr * 64:(pr + 1) * 64]
                                    T.matmul(po, attTG[pr][:, c, :], vpair[(m, pr)][:, c, :])
                                    T.tt(o_allG[:, c, slot - 2, :], po, red[:, 0:64], ALU.add)
                                    yield
                                    if m != "C":
                                        sample_state_update(Sb, eblc[pr][:, 4:20], [(kpairG[pr][:, c, :], vpair[(m, pr)][:, c, :])], pool="G")
                                        T.dma(state_out(m, pr), Sb[:])
                                    elif pr == 1:
                                        sample_state_update(Sb, eblc[pr][:, 4:20],
                                                            [(kpairG[0][:, c, :], vpair[(m, 0)][:, c, :]),
                                                             (kpairG[1][:, c, :], vpair[(m, 1)][:, c, :])], pool="G")
                                        T.dma(state_out(m, pr), Sb[:])
                        if has_s:
                            if m == "C":
                                T.dma(sp_out[m][l].rearrange("h k v -> (h k) v"), Sp[(l, m, 0)][:])
                            else:
                                for pr in range(2):
                                    T.dma(sp_out[m][l, 2 * pr:2 * pr + 2].rearrange("h k v -> (h k) v"), Sp[(l, m, pr)][:])

                    yield
                    for _ in head_norm(o_allG, nstatG, 2, 6, (4, 5), bdtmp[9], next_bdtG, "G"):
                        yield
                gA = gen_A()
                n_chunk, n_front = [0], [0]
                deferred = []
                for (c0, c1, chs) in SLABS:
                    w = c1 - c0
                    sv = load_slab("in", l, c0, 8, w)
                    for (name, cc, cw) in chs:
                        if n_chunk[0] >= 7 and (n_chunk[0] - 7) % 2 == 0 and n_front[0] < 11:
                            next(gA)
                            n_front[0] += 1
                        n_chunk[0] += 1
                        b = bank("st6")
                        for kc in range(8):
                            T.matmul(b[0:cw, 0:nt], sv[:, kc, cc - c0:cc - c0 + cw], xTb[:, kc, 0:nt],
                                     start=(kc == 0), stop=(kc == 7))
                        ps = b[0:cw, 0:nt]
                        kind = name[:2]
                        idx = int(name[2]) if len(name) == 3 and name[2].isdigit() else 0
                        if kind in ("aq", "ak", "av"):
                            j = {"aq": 0, "ak": 2, "av": 4}[kind] + idx
                            T.copy(cb[j][:, 3:3 + nreg], b[:, 0:nreg], eng="act")
                            if has_s:
                                T.copy(cbs[j][:, :, 3:7], b[:, 64:128].rearrange("p (s t) -> p s t", t=4), eng="dve")
                        elif name == "aba":
                            T.copy(ba8[:, 0:nt], ps, eng="dve")
                        elif kind in ("ag", "bg", "cg", "dg"):
                            T.act(sg[name[0].upper()][idx][:, 0:nt], ps, AF.Silu)
                        elif kind == "bq":
                            T.act(qk[("B", "q")][idx][:, 0:nt], ps, AF.Silu)
                        elif kind == "bf":
                            e = tmpG()
                            T.act(e[:, 0:nt], ps, AF.Exp)
                            rl = rlb[:, 2 * l + idx: 2 * l + idx + 1]
                            T.ts(e[:, 0:nt], e[:, 0:nt], rl, ALU.mult, rl, ALU.add)
                            kk_ = qk[("B", "k")][idx]
                            T.recip(kk_[:, 0:nt], e[:, 0:nt])
                            T.ts(e[:, 0:nt], kk_[:, 0:nt], 1.0 - 1e-6, ALU.min, -1.0, ALU.mult, eng="dve")
                            T.act(qk[("B", "g")][idx][:, 0:nt], e[:, 0:nt], AF.Ln, bias=1.0)
                        elif kind in ("bi", "cv", "dv"):
                            m = name[0].upper()
                            pr = idx
                            vb = next_bdt()
                            nch = nt // 64
                            for blk in range(2):
                                if blk == 0:
                                    T.act(vb[:, 0:nch, 0:64], b[:, 0:nt].rearrange("p (c t) -> p c t", t=64),
                                          AF.Copy, scale=C("rm64", 0, 1))
                                else:
                                    T.ts(vb[:, 0:nch, 64:128], b[:, 0:nt].rearrange("p (c t) -> p c t", t=64),
                                         C("rm64", 1, 2), ALU.mult)

                            def _fin_v(vb=vb, m=m, pr=pr, nch=nch):
                                pb = bank("prep")
                                for c in range(nch):
                                    T.matmul(pb[:, c * 64:(c + 1) * 64], vb[:, c, :], I2)
                                T.copy(vpair[(m, pr)][:, 0:nch, :], pb[:, 0:nch * 64].rearrange("p (c v) -> p c v", v=64), eng="act")
                            deferred.append(_fin_v)
                        elif name == "cq":
                            T.act(qk[("C", "q")][0][:, 0:nt], ps, AF.Copy, scale=32.0 ** -0.5)
                        elif name == "ck":
                            T.copy(qk[("C", "k")][0][:, 0:nt], ps, eng="dve")
                        elif name == "clr":
                            T.copy(lrT[:, 0:nt], ps, eng="dve")

                            def _fin_clr():
                                b2 = bank("st6")
                                T.matmul(b2[:, 0:nt], wgc[:, l * 128:(l + 1) * 128], lrT[:, 0:nt])
                                e = tmpG()
                                T.act(e[:, 0:nt], b2[:, 0:nt], AF.Exp, bias=negbg[:, l:l + 1], scale=-1.0)
                                T.act(e[:, 0:nt], e[:, 0:nt], AF.Ln, bias=1.0)
                                T.ts(qk[("C", "g")][0][:, 0:nt], e[:, 0:nt], -1.0 / 16.0, ALU.mult, eng="dve")
                            deferred.append(_fin_clr)
                        elif kind in ("dq", "dk"):
                            raw = vexp[0][:].rearrange("p a b -> p (a b)")[:, ((0 if kind == "dq" else 2) + idx) * 256:((0 if kind == "dq" else 2) + idx + 1) * 256]
                            if kind == "dq":
                                T.copy(raw[:, 0:nt], ps, eng="act")
                            else:
                                T.act(raw[:, 0:nt], ps, AF.Copy, scale=0.125)

                            def _fin_d(raw=raw, kind=kind, idx=idx):
                                b2 = bank("st6")
                                T.matmul(b2[:, 0:nt], C("rot"), raw[:, 0:nt])
                                t1 = tmpG()
                                T.tt(t1[:, 0:nt], raw[:, 0:nt], cosb[:, 0:nt], ALU.mult, eng="dve")
                                t2 = tmpG()
                                T.tt(t2[:, 0:nt], b2[:, 0:nt], sinb[:, 0:nt], ALU.mult)
                                T.tt(qk[("D", kind[1])][idx][:, 0:nt], t1[:, 0:nt], t2[:, 0:nt], ALU.add, eng="dve")
                            deferred.append(_fin_d)
                        else:
                            raise AssertionError(name)
                while n_front[0] < 11:
                    next(gA)
                    n_front[0] += 1
                for fn_ in deferred:
                    fn_()

                if dbg == "proj" and l == 0 and ti in (0, 8):
                    o_ = 0 if ti == 0 else 2048
                    T.dma(dbg_d[:, o_ + 0:o_ + nt], qk[("B", "q")][0][:, 0:nt])
                    T.dma(dbg_d[:, o_ + 256:o_ + 256 + nt], qk[("B", "k")][1][:, 0:nt])
                    T.dma(dbg_d[:, o_ + 512:o_ + 512 + nt], qk[("B", "g")][1][:, 0:nt])
                    T.dma(dbg_d[:, o_ + 768:o_ + 768 + nt], qk[("C", "g")][0][:, 0:nt])
                    T.dma(dbg_d[:, o_ + 1024:o_ + 1024 + nt], qk[("D", "q")][1][:, 0:nt])
                    T.dma(dbg_d[:, o_ + 1280:o_ + 1280 + nt], qk[("D", "k")][0][:, 0:nt])
                    T.dma(dbg_d[:, o_ + 1536:o_ + 1536 + 256], vpair[("B", 1)][:].rearrange("p c v -> p (c v)"))
                    T.dma(dbg_d[:, o_ + 1792:o_ + 1792 + nt], xT[:, 3, 0:nt])

                _gens = [gA, gen_G()]
                while _gens:
                    for g_ in list(_gens):
                        try:
                            next(g_)
                        except StopIteration:
                            _gens.remove(g_)

                if l == nlayers - 1 and ti + 1 < NTILES:
                    nt_next = NT if ti + 1 < 8 else 128
                    for blk_ in range(nt_next // 128):
                        prefetched[(ti + 1, blk_)] = load_block(ti + 1, blk_)

                svs, accs = [], []
                for q4 in range(3):
                    sv = load_slab("out", l, q4 * 256, 8, 256)
                    svs.append(sv)
                    for mc in range(2):
                        g_ = q4 * 2 + mc
                        acc = banks[(0, 1, 2, 5, 6, 7)[g_]][:, 0:nt]
                        accs.append(acc)
                        for kc in range(2, 8):
                            T.matmul(acc, sv[:, kc, mc * 128:(mc + 1) * 128], oT[:, kc, 0:nt],
                                     start=(kc == 2), stop=False)
                for _ in head_norm(o_allA, nstatA, 0, 2, (), bdtmp[0], next_bdt, "prep"):
                    pass
                for q4 in range(3):
                    for mc in range(2):
                        g_ = q4 * 2 + mc
                        for kc in range(2):
                            T.matmul(accs[g_], svs[q4][:, kc, mc * 128:(mc + 1) * 128], oT[:, kc, 0:nt],
                                     start=False, stop=(kc == 1))
                        T.stt(x1T[:, g_, 0:nt], xT[:, g_, 0:nt], ALPHA, accs[g_], ALU.mult, ALU.add)
                for q4 in range(3, 4):
                    sv = load_slab("out", l, q4 * 256, 8, 256)
                    for mc in range(2):
                        b = bank("big")
                        for kc in range(8):
                            T.matmul(b[:, 0:nt], sv[:, kc, mc * 128:(mc + 1) * 128], oT[:, kc, 0:nt],
                                     start=(kc == 0), stop=(kc == 7))
                        T.stt(x1T[:, q4 * 2 + mc, 0:nt], xT[:, q4 * 2 + mc, 0:nt], ALPHA, b[:, 0:nt], ALU.mult, ALU.add)
                layer_norm_fm(x1T, xT, x1Tb, "ln1g", "ln1b", l * 8, nt)

                for sl_ in range(11):
                    svg = load_slab("g", l, sl_ * 256, 8, 256)
                    svu = load_slab("u", l, sl_ * 256, 8, 256)
                    for mc in range(2):
                        bg, bu = bank("big"), bank("big")
                        for kc in range(8):
                            T.matmul(bg[:, 0:nt], svg[:, kc, mc * 128:(mc + 1) * 128], x1Tb[:, kc, 0:nt],
                                     start=(kc == 0), stop=(kc == 7))
                        for kc in range(8):
                            T.matmul(bu[:, 0:nt], svu[:, kc, mc * 128:(mc + 1) * 128], x1Tb[:, kc, 0:nt],
                                     start=(kc == 0), stop=(kc == 7))
                        sgt = tmp()
                        T.act(sgt[:, 0:nt], bg[:, 0:nt], AF.Silu)
                        T.tt(hT[sl_ * 2 + mc][:, 0:nt], bu[:, 0:nt], sgt[:, 0:nt], ALU.mult)
                for mc in range(8):
                    sva = load_slab("d", l, mc * 128, 11, 128, k0=0)
                    svb = load_slab("d", l, mc * 128, 11, 128, k0=11)
                    b = bank("big")
                    for kc in range(22):
                        sv_ = sva if kc < 11 else svb
                        T.matmul(b[:, 0:nt], sv_[:, kc % 11, :], hT[kc][:, 0:nt], start=(kc == 0), stop=(kc == 21))
                    T.stt(x1T[:, mc, 0:nt], xT[:, mc, 0:nt], ALPHA, b[:, 0:nt], ALU.mult, ALU.add)
                layer_norm_fm(x1T, xT, xTb, "ln2g", "ln2b", l * 8, nt)
                if dbg == "x2" and l == 0 and ti in (0, 8):
                    o_ = 0 if ti == 0 else 2048
                    for kc in range(8):
                        T.dma(dbg_d[:, o_ + kc * 256:o_ + kc * 256 + nt], xT[:, kc, 0:nt])

            for blk in range(nt // 128):
                xo = out_stage[blk]
                for kc in range(8):
                    b = bank("big")
                    T.transpose(b[:, 0:128], xT[:, kc, blk * 128:(blk + 1) * 128], ident)
                    T.copy(xo[:, kc * 128:(kc + 1) * 128], b[:, 0:128], eng=("act" if kc % 2 else "dve"))
                c_lo = col0 + blk * 128
                if ti == 0 and blk == 0:
                    T.dma(yp_d[0:64, :], xo[64:128, :])
                elif ti == 8:
                    T.dma(yp_d[SEQ - 64:SEQ, :], xo[0:64, :])
                    T.dma(ys_d[:, :], xo[64:128, :])
                else:
                    T.dma(yp_d[c_lo - 64:c_lo + 64, :], xo[:, :])
        T.finish()
        print(f"[build] instructions={T.n_ins} waits={T.n_wait}")
    return nc


def make_in_maps(inputs):
    consts = make_consts()
    pp128, pp4, pp16 = pack_params(inputs)
    f = lambda a: np.ascontiguousarray(np.asarray(a, np.float32))
    shared = dict(meta=f(inputs["meta_tokens"]), w_in=f(inputs["w_in"]), w_out=f(inputs["w_out"]),
                  w_g=f(inputs["w_ffn_gate"]), w_u=f(inputs["w_ffn_up"]), w_d=f(inputs["w_ffn_down"]),
                  pp128=pp128, pp4=pp4, pp16=pp16, **consts)
    maps = []
    for c in range(8):
        s = slice(16 * c, 16 * c + 16)
        m = dict(shared)
        m["xp"] = f(inputs["x_prompt"][c])
        m["xs"] = f(inputs["x_sample"][s]).reshape(NSAMP, D)
        m["sconv"] = f(inputs["state_delta_conv"][:, s]).reshape(NL, 48, 768)
        m["sdelta"] = f(inputs["state_delta"][:, s])
        m["shgrn"] = f(inputs["state_hgrn"][:, s])
        m["sgla"] = f(inputs["state_gla"][:, s])
        m["sret"] = f(inputs["state_ret"][:, s])
        maps.append(m)
    return maps


def kernel(**inputs):
    nc = build_program()
    maps = make_in_maps(inputs)
    res = run_bass_kernel_spmd(nc, maps, core_ids=list(range(8)))
    r = res.results
    cat = lambda k: np.stack([np.asarray(r[c][k]) for c in range(8)], axis=0)
    y_p = cat("y_p")
    y_s = cat("y_s").reshape(128, 4, D)
    conv_p = cat("conv_p").transpose(1, 0, 2, 3)
    conv_s = np.concatenate([np.asarray(r[c]["conv_s"]) for c in range(8)], axis=1)
    outs = [y_p, y_s, conv_p, conv_s]
    for nm in ("delta", "hgrn", "gla", "ret"):
        outs.append(cat(nm + "_p").transpose(1, 0, 2, 3, 4))
        outs.append(np.concatenate([np.asarray(r[c][nm + "_s"]) for c in range(8)], axis=1))
    return tuple(np.ascontiguousarray(o, dtype=np.float32) for o in outs)
```
